# Optimizing a Trainium2 kernel written in Bass

```python
import math
import jax, jax.numpy as jnp
from jax import lax
import numpy as np

D_MODEL = 1024
BATCH = 2
SEQ = 8192
DEPTH = 1

MIX_WIDTH = D_MODEL
RET_WIDTH = MIX_WIDTH // 2
DIFF_WIDTH = MIX_WIDTH - RET_WIDTH
RET_HD = 64
N_RET_HEADS = RET_WIDTH // RET_HD
DIFF_VD = 128
N_DIFF_HEADS = DIFF_WIDTH // DIFF_VD
DIFF_HD = DIFF_VD // 2
RET_CHUNK = 128
Q_BLOCK = 128
ROPE_BASE = 10000.0
D_FF = ((8 * D_MODEL // 3 + 255) // 256) * 256
NORM_EPS = 1e-6
LAYER_INDEX = 1
LAMBDA_INIT = 0.8 - 0.6 * math.exp(-0.3 * (LAYER_INDEX - 1))
LAMBDA_STD = 0.1
IN_SPLITS = [RET_WIDTH, RET_WIDTH, RET_WIDTH, RET_WIDTH,
             N_DIFF_HEADS * 2 * DIFF_HD,
             N_DIFF_HEADS * 2 * DIFF_HD,
             DIFF_WIDTH]
IN_WIDTH = sum(IN_SPLITS)

kernel_name = "hymba_retnet_diffattn_swiglu"


def rms_norm(x, g):
    xf = x.astype(jnp.float32)
    y = xf * lax.rsqrt(jnp.mean(xf * xf, axis=-1, keepdims=True) + NORM_EPS)
    return (y * g.astype(jnp.float32)).astype(x.dtype)


def rotary(x, pos):
    d = x.shape[-1]
    freqs = 1.0 / (ROPE_BASE ** (jnp.arange(0, d, 2, dtype=jnp.float32) / d))
    ang = pos.astype(jnp.float32)[:, None] * freqs[None, :]
    cos, sin = jnp.cos(ang), jnp.sin(ang)
    xf = x.astype(jnp.float32)
    x1, x2 = xf[..., : d // 2], xf[..., d // 2:]
    return jnp.concatenate([x1 * cos - x2 * sin, x1 * sin + x2 * cos], axis=-1)


def retention_chunkwise(q, k, v, log_g):
    B, H, S, dk = q.shape
    dv = v.shape[-1]
    n = S // RET_CHUNK

    def to_chunks(t):
        return t.reshape(B, H, n, RET_CHUNK, t.shape[-1]).transpose(2, 0, 1, 3, 4)

    idx = jnp.arange(RET_CHUNK, dtype=jnp.float32)
    rel = idx[:, None] - idx[None, :]
    decay_in = jnp.where(rel[None] >= 0,
                         jnp.exp(log_g[:, None, None] * jnp.maximum(rel, 0.0)[None]), 0.0)
    q_dec = jnp.exp(log_g[:, None] * (idx[None, :] + 1.0))
    k_dec = jnp.exp(log_g[:, None] * (RET_CHUNK - 1.0 - idx[None, :]))
    chunk_dec = jnp.exp(log_g * RET_CHUNK)

    def step(state, inp):
        qc, kc, vc = inp
        inner = jnp.einsum('bhid,bhjd->bhij', qc, kc) * decay_in[None]
        o = (jnp.einsum('bhij,bhje->bhie', inner, vc)
             + jnp.einsum('bhid,bhde->bhie', qc * q_dec[None, :, :, None], state))
        new_state = (state * chunk_dec[None, :, None, None]
                     + jnp.einsum('bhjd,bhje->bhde', kc * k_dec[None, :, :, None], vc))
        return new_state, o

    state0 = jnp.zeros((B, H, dk, dv), jnp.float32)
    _, o = lax.scan(step, state0, (to_chunks(q), to_chunks(k), to_chunks(v)))
    return o.transpose(1, 2, 0, 3, 4).reshape(B, H, S, dv)


def diff_attention(q, k, v, lam):
    S = q.shape[3]
    nb = S // Q_BLOCK
    scale = 1.0 / math.sqrt(q.shape[-1])
    kpos = jnp.arange(S)
    kf = k.astype(jnp.float32)
    vf = v.astype(jnp.float32)

    def block(i):
        qb = lax.dynamic_slice_in_dim(q, i * Q_BLOCK, Q_BLOCK, axis=3).astype(jnp.float32)
        s = jnp.einsum('bhmqd,bhmkd->bhmqk', qb, kf) * scale
        qpos = i * Q_BLOCK + jnp.arange(Q_BLOCK)
        mask = kpos[None, :] <= qpos[:, None]
        s = jnp.where(mask[None, None, None], s, -1e30)
        p = jax.nn.softmax(s, axis=-1)
        a = p[:, :, 0] - lam * p[:, :, 1]
        return jnp.einsum('bhqk,bhkd->bhqd', a, vf)

    o = lax.map(block, jnp.arange(nb))
    B, H = q.shape[0], q.shape[1]
    return o.transpose(1, 2, 0, 3, 4).reshape(B, H, S, v.shape[-1])


def setup_inputs(seed: int = 0) -> dict:
    key = jax.random.key(seed)
    ks = jax.random.split(key, 16)
    f32 = jnp.float32
    nrm = lambda k, shape, s: jax.random.normal(k, shape, f32) * s
    gain = lambda k, shape: 1.0 + 0.02 * jax.random.normal(k, shape, f32)
    return {
        "x": jax.random.normal(ks[0], (BATCH, SEQ, D_MODEL), f32),
        "norm1_g": gain(ks[1], (DEPTH, D_MODEL)),
        "w_in": nrm(ks[2], (DEPTH, D_MODEL, IN_WIDTH), D_MODEL ** -0.5),
        "ret_norm_g": gain(ks[3], (DEPTH, N_RET_HEADS, RET_HD)),
        "diff_q_norm_g": gain(ks[4], (DEPTH, DIFF_HD)),
        "diff_k_norm_g": gain(ks[5], (DEPTH, DIFF_HD)),
        "lambda_q1": nrm(ks[6], (DEPTH, DIFF_HD), LAMBDA_STD),
        "lambda_k1": nrm(ks[7], (DEPTH, DIFF_HD), LAMBDA_STD),
        "lambda_q2": nrm(ks[8], (DEPTH, DIFF_HD), LAMBDA_STD),
        "lambda_k2": nrm(ks[9], (DEPTH, DIFF_HD), LAMBDA_STD),
        "diff_subln_g": gain(ks[10], (DEPTH, DIFF_VD)),
        "w_out": nrm(ks[11], (DEPTH, MIX_WIDTH, D_MODEL), MIX_WIDTH ** -0.5),
        "norm2_g": gain(ks[12], (DEPTH, D_MODEL)),
        "w_gate": nrm(ks[13], (DEPTH, D_MODEL, D_FF), D_MODEL ** -0.5),
        "w_up": nrm(ks[14], (DEPTH, D_MODEL, D_FF), D_MODEL ** -0.5),
        "w_down": nrm(ks[15], (DEPTH, D_FF, D_MODEL), D_FF ** -0.5),
    }


def reference(x, norm1_g, w_in, ret_norm_g, diff_q_norm_g, diff_k_norm_g,
              lambda_q1, lambda_k1, lambda_q2, lambda_k2, diff_subln_g,
              w_out, norm2_g, w_gate, w_up, w_down):
    B, S, _ = x.shape
    pos = jnp.arange(S)
    log_g = jnp.log(1.0 - 2.0 ** (-5.0 - jnp.arange(N_RET_HEADS, dtype=jnp.float32)))
    split_pts = list(np.cumsum(IN_SPLITS)[:-1])

    for l in range(DEPTH):
        h = rms_norm(x, norm1_g[l])
        proj = jnp.einsum('bsd,de->bse', h, w_in[l])
        rq, rk, rv, rg, dq, dk, dvv = jnp.split(proj, split_pts, axis=-1)

        heads = lambda t, H, d: t.reshape(B, S, H, d).transpose(0, 2, 1, 3)
        q_r = rotary(heads(rq, N_RET_HEADS, RET_HD), pos)
        k_r = rotary(heads(rk, N_RET_HEADS, RET_HD), pos) * (RET_HD ** -0.5)
        v_r = heads(rv, N_RET_HEADS, RET_HD).astype(jnp.float32)
        ret = retention_chunkwise(q_r, k_r, v_r, log_g)
        ret = ret.transpose(0, 2, 1, 3).astype(x.dtype)
        ret = rms_norm(ret, ret_norm_g[l]).reshape(B, S, RET_WIDTH)
        ret = ret * jax.nn.silu(rg)

        q_d = rms_norm(dq.reshape(B, S, N_DIFF_HEADS, 2, DIFF_HD), diff_q_norm_g[l])
        k_d = rms_norm(dk.reshape(B, S, N_DIFF_HEADS, 2, DIFF_HD), diff_k_norm_g[l])
        q_d = q_d.transpose(0, 2, 3, 1, 4)
        k_d = k_d.transpose(0, 2, 3, 1, 4)
        v_d = heads(dvv, N_DIFF_HEADS, DIFF_VD)
        lam = (jnp.exp(jnp.sum(lambda_q1[l].astype(jnp.float32) * lambda_k1[l].astype(jnp.float32)))
               - jnp.exp(jnp.sum(lambda_q2[l].astype(jnp.float32) * lambda_k2[l].astype(jnp.float32)))
               + LAMBDA_INIT)
        dif = diff_attention(q_d, k_d, v_d, lam)
        dif = dif.transpose(0, 2, 1, 3).astype(x.dtype)
        dif = (rms_norm(dif, diff_subln_g[l]) * (1.0 - LAMBDA_INIT)).reshape(B, S, DIFF_WIDTH)

        mix = jnp.concatenate([ret, dif.astype(ret.dtype)], axis=-1)
        x = x + jnp.einsum('bse,ed->bsd', mix, w_out[l]).astype(x.dtype)

        h2 = rms_norm(x, norm2_g[l])
        ff = jax.nn.silu(jnp.einsum('bsd,df->bsf', h2, w_gate[l])) * jnp.einsum('bsd,df->bsf', h2, w_up[l])
        x = x + jnp.einsum('bsf,fd->bsd', ff, w_down[l]).astype(x.dtype)
    return x
```

```python
import math
import numpy as np
import ml_dtypes
import concourse.bass as bass
import concourse.mybir as mybir
from concourse.bass_utils import run_bass_kernel_spmd

F32 = mybir.dt.float32
BF16 = mybir.dt.bfloat16
AF = mybir.ActivationFunctionType
ALU = mybir.AluOpType
AX = mybir.AxisListType

SEQ = 8192
DM = 1024
NT = SEQ // 128
TSH = 2048
DFF = 2816
NFC = DFF // 128
EPS = 1e-6
import os
RUNAHEAD = int(os.environ.get("K_RUNAHEAD", "8"))
NSLOT = int(os.environ.get("K_NS", "4"))
XCAST_DMA = int(os.environ.get("K_XCAST", "0"))
PRIO_EVAC = int(os.environ.get("K_PRIO", "1"))
SLACK = float(os.environ.get("K_SLACK", "0"))
LAMBDA_INIT = 0.8 - 0.6 * math.exp(-0.3 * 0)
NCORES = 8


class _Op:
    __slots__ = ("stream", "fn", "deps", "order", "dma", "sem", "inc", "sig", "needs", "idx", "busy", "lat", "tag", "t0", "t1", "wr", "rdk", "prio")


def cA(n, acc=False):
    return 300 + 0.8 * n + (226 if acc else 0)


def cD(n, acc=False, fast=False):
    if n <= 8:
        return 280
    return (130 if fast else 200) + (0.55 if fast else 1.0) * n + (85 if acc else 0)


def cP(n):
    return 224 + 2.0 * n


def cPow(n):
    return 420 + 140 * n


def cPE(cols):
    return sum(max(c, 128) * 0.52 + 6 for c in cols)


class Sched:
    STREAMS = ("pe", "act", "dve", "pool", "sp")

    def __init__(self, nc, n_dma_sems=6):
        self.nc = nc
        self.ops = []
        self.keys = {}
        self.overl = {}
        self.lastw = {}
        self.rd_c = {}
        self.rd_d = {}
        self.sem = {s: nc.alloc_semaphore("sem_" + s) for s in self.STREAMS}
        self.dsems = {}
        self.drr = {}
        self.sem_last = {}
        self.ccsems = [nc.alloc_semaphore(f"ccsem{i}") for i in range(4)]
        self.ccrr = 0

    def reg(self, key, space, lo, hi):
        assert key not in self.keys, key
        ov = []
        for k, (sp, l, h) in self.keys.items():
            if sp == space and l < hi and lo < h:
                ov.append(k)
                self.overl[k].append(key)
        self.keys[key] = (space, lo, hi)
        self.overl[key] = ov + [key]

    def add(self, stream, fn, reads=(), writes=(), xreads=(), dma=False, cc=False, cost=300.0, lat=None, semgrp=("misc", 3), prio=None):
        op = _Op()
        op.wr = tuple(writes) + tuple(xreads)
        op.prio = prio if prio is not None else (0 if (len(xreads) > 0 and PRIO_EVAC) else 1)
        op.rdk = tuple(reads)
        op.stream, op.fn, op.dma = stream, fn, (dma or cc)
        op.deps, op.order, op.needs, op.sig = {}, {}, False, None
        op.idx = len(self.ops)
        op.tag = getattr(self, "tag", "") + str(getattr(self, "tile", ""))
        op.busy = float(cost)
        op.lat = float(lat) if lat is not None else float(cost)

        def dep(o, sync_same=True):
            if o is None or o is op:
                return
            if (not op.dma) and (not o.dma) and o.stream == stream:
                if stream == "pe" or not sync_same:
                    op.order[o.idx] = o
                    return
            op.deps[o.idx] = o

        for k in reads:
            for kk in self.overl[k]:
                dep(self.lastw.get(kk))
        for k in xreads:
            for kk in self.overl[k]:
                dep(self.lastw.get(kk))
                for r in self.rd_c.get(kk, ()):
                    dep(r, sync_same=False)
                for r in self.rd_d.get(kk, ()):
                    dep(r)
        for k in writes:
            for kk in self.overl[k]:
                dep(self.lastw.get(kk))
                for r in self.rd_c.get(kk, ()):
                    dep(r)
                for r in self.rd_d.get(kk, ()):
                    dep(r)
        if cc:
            op.sem, op.inc = self.ccsems[self.ccrr % len(self.ccsems)], 1
            self.ccrr += 1
            dep(self.sem_last.get(id(op.sem)))
            self.sem_last[id(op.sem)] = op
        elif dma:
            gk = (stream, semgrp[0])
            if gk not in self.dsems:
                self.dsems[gk] = [self.nc.alloc_semaphore(f"dsem_{stream}_{semgrp[0]}{i}") for i in range(semgrp[1])]
                self.drr[gk] = 0
            sems = self.dsems[gk]
            op.sem, op.inc = sems[self.drr[gk] % len(sems)], 16
            self.drr[gk] += 1
            dep(self.sem_last.get(id(op.sem)))
            self.sem_last[id(op.sem)] = op
        else:
            op.sem, op.inc = self.sem[stream], 1
        for k in tuple(reads) + tuple(xreads):
            if op.dma:
                self.rd_d.setdefault(k, []).append(op)
            else:
                self.rd_c.setdefault(k, []).append(op)
        for k in writes:
            self.lastw[k] = op
            self.rd_c[k] = []
            self.rd_d[k] = []
        self.ops.append(op)
        return op

    def schedule(self, reorder=True):
        ops = self.ops
        n = len(ops)
        if not reorder:
            return {s: [o for o in ops if o.stream == s] for s in self.STREAMS}
        HOP = 150.0
        succ = [[] for _ in range(n)]
        indeg = [0] * n
        for o in ops:
            for p in o.deps.values():
                succ[p.idx].append((o.idx, HOP))
                indeg[o.idx] += 1
            for p in o.order.values():
                if p.idx not in o.deps:
                    succ[p.idx].append((o.idx, 0.0))
                    indeg[o.idx] += 1
        dready = [0.0] * n
        ready = {s: [] for s in self.STREAMS}
        for o in ops:
            if indeg[o.idx] == 0:
                ready[o.stream].append(o.idx)
        free = {s: 0.0 for s in self.STREAMS}
        out = {s: [] for s in self.STREAMS}
        done = 0
        tend = 0.0
        while done < n:
            best = None
            for s in self.STREAMS:
                r = ready[s]
                if not r:
                    continue
                f = free[s]
                bi = None
                for i in r:
                    st = dready[i] if dready[i] > f else f
                    key = (st if st > f + SLACK else f, ops[i].prio, i)
                    if bi is None or key < bi:
                        bi = key
                if best is None or bi < best[0]:
                    best = (bi, s)
            (_, _, i), s = best
            st = dready[i] if dready[i] > free[s] else free[s]
            ready[s].remove(i)
            o = ops[i]
            free[s] = st + o.busy
            fin = st + o.lat
            o.t0, o.t1 = st, fin
            tend = max(tend, fin)
            out[s].append(o)
            done += 1
            for j, hop in succ[i]:
                t = fin + hop
                if t > dready[j]:
                    dready[j] = t
                indeg[j] -= 1
                if indeg[j] == 0:
                    ready[ops[j].stream].append(j)
        self.est_ns = tend
        if getattr(self, "verbose", False):
            tags = []
            base = lambda s: s.rstrip("0123456789")
            for o in ops:
                if base(o.tag) not in tags:
                    tags.append(base(o.tag))
            for tg in tags:
                sel = [o for o in ops if base(o.tag) == tg]
                b = {s: sum(o.busy for o in sel if o.stream == s) for s in self.STREAMS}
                print(f"  [{tg:8s}] t=[{min(o.t0 for o in sel) / 1e3:8.1f},{max(o.t1 for o in sel) / 1e3:8.1f}] us  busy(us): " +
                      " ".join(f"{s}={b[s] / 1e3:7.1f}" for s in self.STREAMS))
        return out

    def emit(self, reorder=True):
        nc = self.nc
        per = self.schedule(reorder)
        cnt = {}
        for op in self.ops:
            if op.dma:
                c = cnt.get(id(op.sem), 0) + op.inc
                cnt[id(op.sem)] = c
                op.sig = c
        pos = {}
        for s in self.STREAMS:
            for i, op in enumerate(per[s]):
                pos[op.idx] = i
        for op in self.ops:
            latest = {}
            for d in op.deps.values():
                if d.dma:
                    continue
                if d.stream not in latest or pos[d.idx] > pos[latest[d.stream].idx]:
                    latest[d.stream] = d
            for d in latest.values():
                d.needs = True
            op.deps = {d.idx: d for d in op.deps.values() if d.dma or latest[d.stream] is d}
        for s in self.STREAMS:
            c = 0
            for op in per[s]:
                if (not op.dma) and op.needs:
                    c += 1
                    op.sig = c

        if reorder:
            order = sorted(self.ops, key=lambda o: (o.t0, o.idx))
        else:
            order = list(self.ops)
        cur = {s: {} for s in self.STREAMS}
        clk = {}
        waits = {}
        for op in order:
            c = cur[op.stream]
            need = {}
            for d in op.deps.values():
                k = id(d.sem)
                if d.sig > need.get(k, (0, None, None))[0]:
                    need[k] = (d.sig, d.sem, d)
            wl = []
            for k, (v, sem, d) in sorted(need.items(), key=lambda kv: -kv[1][2].t1 if reorder else 0):
                if c.get(k, 0) >= v:
                    continue
                wl.append((sem, v))
                c[k] = v
                for kk, vv in clk[d.idx].items():
                    if vv > c.get(kk, 0):
                        c[kk] = vv
            waits[op.idx] = wl
            mine = dict(c)
            if op.sig is not None:
                mine[id(op.sem)] = max(mine.get(id(op.sem), 0), op.sig)
                if not op.dma:
                    c[id(op.sem)] = max(c.get(id(op.sem), 0), 0)
            clk[op.idx] = mine

        def run(eng, ops):
            for op in ops:
                for sem, v in waits[op.idx]:
                    eng.wait_ge(sem, v)
                ins = op.fn(eng)
                if op.sig is not None:
                    ins.then_inc(op.sem, op.inc)

        with nc.Block() as block:
            @block.tensor
            def _(e):
                run(e, per["pe"])

            @block.scalar
            def _(e):
                run(e, per["act"])

            @block.vector
            def _(e):
                run(e, per["dve"])

            @block.gpsimd
            def _(e):
                run(e, per["pool"])

            @block.sync
            def _(e):
                run(e, per["sp"])


class Arena:
    def __init__(self, sched, ap, total):
        self.s, self.ap, self.total, self.off = sched, ap, total, 0

    def _take(self, words, key):
        lo = self.off
        self.off += words
        assert self.off <= self.total, (key, self.off, self.total)
        if key is not None:
            self.s.reg(key, "sb", lo, lo + words)
        return self.ap[:, lo:lo + words]

    def f32(self, n, key):
        return self._take(n, key)

    def bf(self, n, key):
        assert n % 2 == 0
        return self._take(n // 2, key).bitcast(BF16)

    def sub(self, key, parent_lo_words, words):
        self.s.reg(key, "sb", parent_lo_words, parent_lo_words + words)


def build_nc(stop_after=None, reorder=True):
    nc = bass.Bass("TRN2", target_bir_lowering=False)
    S = Sched(nc)
    S.verbose = True
    S.tag = "setup"

    def din(name, shape, dt=F32):
        return nc.dram_tensor(name, list(shape), dt, kind="ExternalInput").ap()

    x_d = din("x", [SEQ, DM])
    xres_d = din("xres", [TSH, DM])
    win_d = din("w_in", [DM, 896])
    wout_d = din("w_out", [DM, DM])
    wg_d = din("w_gate", [DM, DFF])
    wu_d = din("w_up", [DM, DFF])
    wd_d = din("w_down", [DFF, DM])
    g1_d = din("g1col", [128, 8])
    g2_d = din("g2col", [128, 8])
    cos_d = din("cos_t", [128, NT * 32])
    sin_d = din("sin_t", [128, NT * 64])
    kod_d = din("kod", [128, 4])
    cvec_d = din("cvec", [128, 1])
    cm2_d = din("cmask2", [128, 256])
    gret_d = din("gret", [128, 128])
    gqk_d = din("gqk", [128, 256])
    gsub_d = din("gsub", [128, 128])
    lam_d = din("lam4", [128, 256])
    ident_d = din("ident", [128, 128], BF16)
    cmask_d = din("cmask", [128, 128], BF16)
    if stop_after is None:
        y_d = nc.dram_tensor("y", [TSH, DM], F32, kind="ExternalOutput").ap()
    else:
        y_d = nc.dram_tensor("y", [256, SEQ], BF16, kind="ExternalOutput").ap()
    mixsrc_t = nc.dram_tensor("mixsrc", [16 * 256, 512], BF16)
    gath_t = nc.dram_tensor("gath", [16 * 1024, 512], BF16)
    mixsrc = mixsrc_t.ap()
    gath = gath_t.ap()
    for k in ("x", "xres", "w_in", "w_out", "w_gate", "w_up", "w_down", "const"):
        S.reg("d:" + k, "dram_" + k, 0, 1)
    for G in range(16):
        S.reg(f"d:ms{G}", "dram_ms", G, G + 1)
        S.reg(f"d:gath{G}", "dram_gath", G, G + 1)
    for g in range(8):
        S.reg(f"d:y{g}", "dram_y", g, g + 1)

    TOTAL = 49152
    arena_t = nc.alloc_sbuf_tensor("arena", [128, TOTAL], F32)
    A = Arena(S, arena_t.ap(), TOTAL)
    ps_t = nc.alloc_psum_tensor("psum_all", [128, 4096], F32)
    PS = ps_t.ap()
    for b in range(8):
        S.reg(f"ps{b}", "ps", b, b + 1)
        S.reg(f"done{b}", "marker", b, b + 1)

    def bank(b, c0=0, c1=512):
        return PS[:, b * 512 + c0: b * 512 + c1]

    def bank_bf(b):
        return PS[:, b * 512:(b + 1) * 512].bitcast(BF16)

    def dma(stream, out, in_, reads, writes, nbytes=65536, grp=("misc", 3)):
        if stream == "sp":
            return S.add(stream, lambda e: e.dma_start(out=out, in_=in_), reads=reads, writes=writes, dma=True,
                         cost=70, lat=2000 + nbytes / 150.0, semgrp=grp)
        return S.add(stream, lambda e: e.dma_start(out=out, in_=in_), reads=reads, writes=writes, dma=True,
                     cost=900, lat=2500 + nbytes / 100.0, semgrp=grp)

    ident = A.bf(128, "ident")
    cmask = A.bf(128, "cmask")
    small = A.f32(64, "small")
    neghalf = small[:, 0:1]
    lam_e = small[:, 1:3]
    neglam = small[:, 3:4]
    negM = small[:, 4:5]
    gmax = small[:, 5:7]
    lsum = small[:, 7:9]
    PERSIST_END = A.off

    NS = NSLOT
    gsub = A.f32(128, "gsub")
    lam4 = A.f32(256, "lam4")
    lamtmp = A.f32(256, "lamtmp")
    kdT_lo = A.off
    kdT = A.bf(SEQ, None)
    vd_lo = A.off
    vd = A.bf(NT * 130, None)
    vd3 = vd.rearrange("p (t c) -> p t c", c=130)
    for t in range(NT):
        A.sub(f"kdT{t}", kdT_lo + t * 64, 64)
        A.sub(f"vd{t}", vd_lo + t * 65, 65)
    PT = [A.bf(1024, f"PT{i}") for i in range(2)]
    Osb = A.f32(8 * 130, "Osb")
    rl = A.f32(8, "rl")
    coef = A.f32(4, "coef")
    dtmp = A.f32(128, "dtmp")
    dsq = A.f32(128, "dsq")
    ssd = A.f32(1, "ssd")
    rd = A.f32(1, "rd")
    difn = A.bf(128, "difn")
    mdifR = [A.bf(512, f"mdifR{i}") for i in range(2)]
    mretR = [A.bf(512, f"mretR{i}") for i in range(2)]
    L_END = A.off
    win_b = A.bf(8 * 896, "win")
    win3 = win_b.rearrange("p (k n) -> p k n", k=8)
    g1col = A.f32(8, "g1col")
    NX = 3
    xt = [A.f32(DM + 1, f"xt{i}") for i in range(NX)]
    junk = A.bf(DM + 2, "junk")
    NXB = 3
    xb = [A.bf(DM, f"xb{i}") for i in range(NXB)]
    xT = [A.bf(DM, f"xT{i}") for i in range(2)]
    cos_t = A.f32(NT * 32, "cos_t")
    sin_t = A.f32(NT * 64, "sin_t")
    kod = A.f32(4, "kod")
    cvec = A.f32(1, "cvec")
    cmask2 = A.f32(256, "cmask2")
    gret = A.f32(128, "gret")
    gqk = A.f32(256, "gqk")
    St = A.f32(64, "St")
    Sb = A.bf(64, "Sb")

    def two(fn, n, name):
        return [fn(n, f"{name}{i}") for i in range(NS)]
    ss = two(A.f32, 1, "ss")
    rstd = two(A.f32, 1, "rstd")
    nrm = two(A.f32, 6, "nrm")
    comb = two(A.f32, 8, "comb")
    sk = two(A.f32, 8, "sk")
    t1 = two(A.f32, 256, "t1")
    t2 = []
    for i in range(NS):
        lo_ = A.off
        t2.append(A.f32(256, None))
        A.sub(f"t2a{i}", lo_, 128)
        A.sub(f"t2b{i}", lo_ + 128, 128)
    rk = two(A.f32, 128, "rk")
    qb = two(A.bf, 256, "qb")
    kb = two(A.bf, 128, "kb")
    vb = two(A.bf, 128, "vb")
    gate = two(A.f32, 128, "gate")
    gg = two(A.f32, 128, "gg")
    qkT = two(A.bf, 384, "qkT")
    Mm = two(A.bf, 256, "Mm")
    o_sb = two(A.f32, 130, "o_sb")
    osq, ss2 = [], []
    for i in range(NS):
        lo_ = A.off
        osq.append(A.f32(130, None))
        A.sub(f"osq{i}h0", lo_, 65)
        A.sub(f"osq{i}h1", lo_ + 65, 65)
        lo_ = A.off
        ss2.append(A.f32(2, None))
        A.sub(f"ss2{i}h0", lo_, 1)
        A.sub(f"ss2{i}h1", lo_ + 1, 1)
    r2 = two(A.f32, 2, "r2")
    otmp = two(A.f32, 128, "otmp")
    mixr = two(A.bf, 128, "mixr")
    sqd = two(A.f32, 256, "sqd")
    qkraw = two(A.f32, 256, "qkraw")
    msd = two(A.f32, 4, "msd")
    tmpd = two(A.f32, 256, "tmpd")
    qkh = two(A.bf, 256, "qkh")
    WA_END = A.off
    assert WA_END <= TOTAL - 7168, WA_END
    A.off = TOTAL - 7168
    winf_lo = A.off
    winf = A.f32(8 * 896, None)
    for kc in range(8):
        A.sub(f"winf{kc}", winf_lo + kc * 896, 896)
    A.off = TOTAL - 4096
    qdT_lo = A.off
    qdT = A.bf(SEQ, None)
    for t in range(NT):
        A.sub(f"qdT{t}", qdT_lo + t * 64, 64)
    A.off = PERSIST_END
    g2col = A.f32(8, "g2col")
    x1 = [A.f32(2 * DM, f"x1_{i}") for i in range(2)]
    mixg = A.bf(8 * 256, "mixg")
    h2 = A.bf(DM, "h2")
    h2T = [A.bf(8 * 256, f"h2T{i}") for i in range(2)]
    ffT = A.bf(NFC * 256, "ffT")
    sg = [A.f32(256, f"sg{i}") for i in range(2)]
    ssc = A.f32(2, "ssc")
    rsc = A.f32(2, "rsc")
    CA_END = A.off
    WBASE = TOTAL - (NFC * DM + 8 * DM + 2 * 8 * DFF) // 2
    assert CA_END <= WBASE, (CA_END, WBASE)
    A.off = WBASE
    wd_lo = A.off
    wd_b = A.bf(NFC * DM, None)
    for fc in range(NFC):
        A.sub(f"wd{fc}", wd_lo + (NFC - 1 - fc) * 512, 512)
    wo_b = A.bf(8 * DM, "wo")
    wg_lo = A.off
    wg_b = A.bf(8 * DFF, None)
    wu_lo = A.off
    wu_b = A.bf(8 * DFF, None)
    for kc in range(8):
        A.sub(f"wg{kc}", wg_lo + kc * (DFF // 2), DFF // 2)
        A.sub(f"wu{kc}", wu_lo + kc * (DFF // 2), DFF // 2)
    assert A.off == TOTAL, A.off
    print("arena: L_END", L_END, "WA_END", WA_END, "CA_END", CA_END, "WBASE", WBASE)
    WG_ALL = [f"wg{kc}" for kc in range(8)]
    WU_ALL = [f"wu{kc}" for kc in range(8)]
    WD_ALL = [f"wd{fc}" for fc in range(NFC)]

    winf3 = winf.rearrange("p (k n) -> p k n", k=8)
    win_src = win_d.rearrange("(k p) n -> p k n", p=128)
    dma("sp", g1col, g1_d, ["d:const"], ["g1col"])
    for kc in range(8):
        dma("sp", winf3[:, kc, :], win_src[:, kc, :], ["d:w_in"], [f"winf{kc}"], nbytes=128 * 896 * 4, grp=("win", 8))
    dma("sp", ident, ident_d, ["d:const"], ["ident"])
    dma("sp", cmask, cmask_d, ["d:const"], ["cmask"])
    dma("sp", kod, kod_d, ["d:const"], ["kod"])
    dma("sp", cvec, cvec_d, ["d:const"], ["cvec"])
    dma("sp", cmask2, cm2_d, ["d:const"], ["cmask2"])
    dma("sp", gret, gret_d, ["d:const"], ["gret"])
    dma("sp", gqk, gqk_d, ["d:const"], ["gqk"])
    dma("sp", cos_t, cos_d, ["d:const"], ["cos_t"], nbytes=128 * NT * 32 * 4)
    dma("sp", sin_t, sin_d, ["d:const"], ["sin_t"], nbytes=128 * NT * 64 * 4)
    dma("sp", gsub, gsub_d, ["d:const"], ["gsub"])
    dma("sp", lam4, lam_d, ["d:const"], ["lam4"])
    for kc in range(8):
        if kc % 2 == 0:
            S.add("dve", lambda e, kc=kc: e.tensor_scalar(out=win3[:, kc, :], in0=winf3[:, kc, :], scalar1=g1col[:, kc:kc + 1],
                                                         scalar2=None, op0=ALU.mult), reads=[f"winf{kc}", "g1col"], writes=["win"], cost=cD(896))
        else:
            S.add("act", lambda e, kc=kc: e.activation(out=win3[:, kc, :], in_=winf3[:, kc, :], func=AF.Copy, scale=g1col[:, kc:kc + 1]),
                  reads=[f"winf{kc}", "g1col"], writes=["win"], cost=cA(896))

    S.add("pool", lambda e: e.memset(neghalf, -0.5), writes=["small"], cost=150)
    S.add("pool", lambda e: e.memset(St, 0.0), writes=["St"], cost=150)
    S.add("pool", lambda e: e.memset(Sb, 0.0), writes=["Sb"], cost=150)
    for i in range(NS):
        S.add("pool", lambda e, i=i: e.memset(qb[i], 0.0), writes=[f"qb{i}"], cost=300)
        S.add("pool", lambda e, i=i: e.memset(o_sb[i].rearrange("p (h c) -> p h c", h=2)[:, :, 64:65], math.sqrt(64 * EPS)),
              writes=[f"o_sb{i}"], cost=150)
        dma("sp", comb[i][:, 4:8], kod_d, ["d:const"], [f"comb{i}"])
    for i in range(NX):
        S.add("pool", lambda e, i=i: e.memset(xt[i][:, DM:DM + 1], 32.0 * math.sqrt(EPS)), writes=[f"xt{i}"], cost=150)

    S.add("pool", lambda e: e.memset(vd3[:, :, 128:130], 1.0), writes=[f"vd{t}" for t in range(NT)], cost=500)
    l4 = lam4.rearrange("p (a b) -> p a b", b=64)
    lt3 = lamtmp.rearrange("p (a b) -> p a b", b=64)
    S.add("dve", lambda e: e.tensor_tensor(out=lt3[:, 0:2, :], in0=l4[:, 0:2, :], in1=l4[:, 2:4, :], op=ALU.mult),
          reads=["lam4"], writes=["lamtmp"])
    S.add("dve", lambda e: e.tensor_reduce(out=lsum, in_=lt3[:, 0:2, :], axis=AX.X, op=ALU.add),
          reads=["lamtmp"], writes=["small"])
    S.add("act", lambda e: e.activation(out=lam_e, in_=lsum, func=AF.Exp), reads=["small"], writes=["small"])
    S.add("dve", lambda e: e.tensor_tensor(out=neglam, in0=lam_e[:, 1:2], in1=lam_e[:, 0:1], op=ALU.subtract),
          reads=["small"], writes=["small"])
    S.add("dve", lambda e: e.tensor_scalar(out=neglam, in0=neglam, scalar1=-LAMBDA_INIT, scalar2=None, op0=ALU.add),
          reads=["small"], writes=["small"])
    gq3 = gqk.rearrange("p (a b) -> p a b", b=128)
    S.add("dve", lambda e: e.tensor_reduce(out=gmax, in_=gq3, axis=AX.X, op=ALU.max, apply_absolute_value=True),
          reads=["gqk"], writes=["small"])
    S.add("dve", lambda e: e.tensor_tensor(out=negM, in0=gmax[:, 0:1], in1=gmax[:, 1:2], op=ALU.mult),
          reads=["small"], writes=["small"])
    S.add("dve", lambda e: e.tensor_scalar(out=negM, in0=negM, scalar1=-8.0, scalar2=None, op0=ALU.mult),
          reads=["small"], writes=["small"])
    S.add("dve", lambda e: e.tensor_scalar(out=gsub, in0=gsub, scalar1=1.0 - LAMBDA_INIT, scalar2=None, op0=ALU.mult),
          reads=["gsub"], writes=["gsub"])
    S.add("dve", lambda e: e.tensor_scalar(out=gqk, in0=gqk, scalar1=8.0, scalar2=None, op0=ALU.mult), reads=["gqk"], writes=["gqk"],
          cost=cD(256))
    S.add("dve", lambda e: e.tensor_scalar(out=gret, in0=gret, scalar1=8.0, scalar2=None, op0=ALU.mult), reads=["gret"], writes=["gret"],
          cost=cD(128))

    B_XT, B_A, B_B, B_T2, B_S0, B_S1, B_O, B_A2 = 0, 1, 2, 7, 4, 5, 6, 3
    B_T3 = B_T2
    psT = bank_bf(B_XT)
    psA = bank(B_A)
    psA2 = bank(B_A2)
    psB = bank(B_B)
    psT2 = bank_bf(B_T2)
    psT3 = bank_bf(B_T2)[:, 384:640]
    psS = [bank(B_S0), bank(B_S1)]
    psO = bank(B_O)
    psMX = bank_bf(B_S1)
    cos3 = cos_t.rearrange("p (t f) -> p t f", f=32)
    sin4 = sin_t.rearrange("p (t h f) -> p t h f", h=2, f=32)

    def phaseA_tile(t):
        s = t % 2
        sx = t % NX
        u = t % NS
        G = t // 4
        dma("sp", xt[sx][:, 0:DM], x_d[t * 128:(t + 1) * 128, :], ["d:x"] + ([f"done{(t - RUNAHEAD) % 8}"] if t >= RUNAHEAD else []),
            [f"xt{sx}"], nbytes=128 * 4096, grp=("x", NX))
        S.add("act", lambda e: e.activation(out=junk[:, 0:DM + 1], in_=xt[sx], func=AF.Square, scale=1.0 / 32.0, accum_out=ss[u]),
              reads=[f"xt{sx}"], writes=["junk", f"ss{u}"], cost=cA(1025, True))
        sb_ = t % NXB
        if XCAST_DMA:
            dma("pool", xb[sb_], xt[sx][:, 0:DM], [f"xt{sx}"], [f"xb{sb_}"], nbytes=128 * 4096, grp=("xc", NXB))
        else:
            S.add("act", lambda e: e.activation(out=xb[sb_], in_=xt[sx][:, 0:DM], func=AF.Copy), reads=[f"xt{sx}"], writes=[f"xb{sb_}"],
                  cost=cA(1024))
        S.add("pool", lambda e: e.tensor_tensor(out=rstd[u], in0=ss[u], in1=neghalf, op=ALU.pow),
              reads=[f"ss{u}", "small"], writes=[f"rstd{u}"], cost=cPow(1))
        rs = rstd[u]

        def tr_x(e):
            ins = None
            for kc in range(8):
                ins = e.transpose(psT[:, kc * 128:(kc + 1) * 128], xb[sb_][:, kc * 128:(kc + 1) * 128], ident)
            return ins
        S.add("pe", tr_x, reads=[f"xb{sb_}", "ident"], writes=[f"ps{B_XT}"], cost=8 * 75)
        S.add("dve", lambda e: e.tensor_copy(out=xT[s], in_=psT), reads=[], writes=[f"xT{s}"], xreads=[f"ps{B_XT}"], cost=cD(1024, fast=True))
        xT3 = xT[s].rearrange("p (k n) -> p k n", k=8)

        def inprojA(e):
            ins = None
            for kc in range(8):
                ins = e.matmul(psA[:, 0:256], lhsT=xT3[:, kc, :], rhs=win3[:, kc, 0:256], start=(kc == 0), stop=(kc == 7))
            return ins

        def inprojA2(e):
            ins = None
            for kc in range(8):
                ins = e.matmul(psA2[:, 0:256], lhsT=xT3[:, kc, :], rhs=win3[:, kc, 256:512], start=(kc == 0), stop=(kc == 7))
            return ins

        def inprojB(e):
            ins = None
            for kc in range(8):
                ins = e.matmul(psB[:, 0:384], lhsT=xT3[:, kc, :], rhs=win3[:, kc, 512:896], start=(kc == 0), stop=(kc == 7))
            return ins
        S.add("pe", inprojA, reads=[f"xT{s}", "win"], writes=[f"ps{B_A}"], cost=cPE([256] * 8))
        S.add("pe", inprojA2, reads=[f"xT{s}", "win"], writes=[f"ps{B_A2}"], cost=cPE([256] * 8))
        S.add("pe", inprojB, reads=[f"xT{s}", "win"], writes=[f"ps{B_B}"], cost=cPE([384] * 8))
        S.add("act", lambda e: e.activation(out=qkraw[u], in_=psB[:, 0:256], func=AF.Copy), reads=[], writes=[f"qkraw{u}"], xreads=[f"ps{B_B}"],
              cost=cA(256))
        S.add("pool", lambda e: e.tensor_tensor(out=sqd[u], in0=qkraw[u], in1=qkraw[u], op=ALU.mult), reads=[f"qkraw{u}"], writes=[f"sqd{u}"],
              cost=cP(256))
        S.add("act", lambda e: e.activation(out=vd3[:, t, 0:128], in_=psB[:, 256:384], func=AF.Copy, scale=rs),
              reads=[f"rstd{u}"], writes=[f"vd{t}"], xreads=[f"ps{B_B}"], cost=cA(128))
        S.add("dve", lambda e: e.tensor_reduce(out=msd[u], in_=sqd[u].rearrange("p (a d) -> p a d", d=64), axis=AX.X, op=ALU.add),
              reads=[f"sqd{u}"], writes=[f"msd{u}"], cost=410)
        S.add("dve", lambda e: e.reciprocal(out=nrm[u][:, 0:1], in_=ss[u]), reads=[f"ss{u}"], writes=[f"nrm{u}"], cost=160)
        S.add("dve", lambda e: e.tensor_scalar(out=nrm[u][:, 1:5], in0=msd[u], scalar1=nrm[u][:, 0:1], scalar2=64.0 * EPS,
                                               op0=ALU.mult, op1=ALU.add), reads=[f"msd{u}", f"nrm{u}"], writes=[f"nrm{u}"], cost=cD(4))
        S.add("pool", lambda e: e.tensor_tensor(out=comb[u][:, 0:4], in0=nrm[u][:, 1:5], in1=neghalf.broadcast_to([128, 4]), op=ALU.pow),
              reads=[f"nrm{u}", "small"], writes=[f"comb{u}"], cost=cPow(4))
        S.add("dve", lambda e: e.tensor_scalar(out=sk[u], in0=comb[u], scalar1=rs, scalar2=None, op0=ALU.mult),
              reads=[f"comb{u}", f"rstd{u}"], writes=[f"sk{u}"], cost=cD(8))
        X8 = psA[:, 0:256].rearrange("p (a f) -> p a f", f=32)
        X4 = psA[:, 0:256].rearrange("p (a h f) -> p a h f", h=2, f=32)
        t1_8 = t1[u].rearrange("p (a f) -> p a f", f=32)
        t2h = t2[u].rearrange("p (h a f) -> p h a f", h=2, f=32)
        t2v = t2[u].rearrange("p (h a f) -> p a h f", h=2, f=32)
        cb = cos3[:, t, :].unsqueeze(1).broadcast_to([128, 8, 32])
        nsb = sin4[:, t, 0, :].unsqueeze(1).broadcast_to([128, 4, 32])
        psb = sin4[:, t, 1, :].unsqueeze(1).broadcast_to([128, 4, 32])
        S.add("dve", lambda e: e.tensor_tensor(out=t1_8, in0=X8, in1=cb, op=ALU.mult),
              reads=["cos_t"], writes=[f"t1{u}"], xreads=[f"ps{B_A}"], cost=cD(256))
        S.add("dve", lambda e: e.tensor_tensor(out=t2h[:, 0, :, :], in0=X4[:, :, 1, :], in1=nsb, op=ALU.mult),
              reads=["sin_t"], writes=[f"t2a{u}"], xreads=[f"ps{B_A}"], cost=cD(128))
        S.add("dve", lambda e: e.tensor_tensor(out=t2h[:, 1, :, :], in0=X4[:, :, 0, :], in1=psb, op=ALU.mult),
              reads=["sin_t"], writes=[f"t2b{u}"], xreads=[f"ps{B_A}"], cost=cD(128))
        S.add("act", lambda e: e.activation(out=vb[u], in_=psA2[:, 0:128], func=AF.Copy, scale=rs),
              reads=[f"rstd{u}"], writes=[f"vb{u}"], xreads=[f"ps{B_A2}"], cost=cA(128))
        S.add("act", lambda e: e.activation(out=gate[u], in_=psA2[:, 128:256], func=AF.Silu, scale=rs),
              reads=[f"rstd{u}"], writes=[f"gate{u}"], xreads=[f"ps{B_A2}"], cost=cA(128))
        t1v = t1[u].rearrange("p (a h f) -> p a h f", h=2, f=32)
        S.add("dve", lambda e: e.tensor_tensor(out=qb[u].rearrange("p (a h f) -> p a h f", h=2, f=32)[:, 0:4:3, :, :],
                                               in0=t1v[:, 0:2, :, :], in1=t2v[:, 0:2, :, :], op=ALU.add),
              reads=[f"t1{u}", f"t2a{u}", f"t2b{u}"], writes=[f"qb{u}"], cost=cD(128))
        S.add("pool", lambda e: e.tensor_tensor(out=rk[u].rearrange("p (a h f) -> p a h f", h=2, f=32), in0=t1v[:, 2:4, :, :],
                                                in1=t2v[:, 2:4, :, :], op=ALU.add),
              reads=[f"t1{u}", f"t2a{u}", f"t2b{u}"], writes=[f"rk{u}"], cost=cP(128))
        S.add("dve", lambda e: e.tensor_tensor(out=kb[u].rearrange("p (h d) -> p h d", h=2), in0=rk[u].rearrange("p (h d) -> p h d", h=2),
                                               in1=sk[u][:, 4:6].unsqueeze(2).broadcast_to([128, 2, 64]), op=ALU.mult),
              reads=[f"rk{u}", f"sk{u}"], writes=[f"kb{u}"], cost=cD(128))
        S.add("pool", lambda e: e.tensor_tensor(out=gg[u], in0=gate[u], in1=gret, op=ALU.mult), reads=[f"gate{u}", "gret"],
              writes=[f"gg{u}"], cost=cP(128))

        def tr_qk(e):
            e.transpose(psT2[:, 0:128], qb[u][:, 0:128], ident)
            e.transpose(psT2[:, 128:256], qb[u][:, 128:256], ident)
            return e.transpose(psT2[:, 256:384], kb[u], ident)
        S.add("pe", tr_qk, reads=[f"qb{u}", f"kb{u}", "ident"], writes=[f"ps{B_T2}"], cost=225)
        S.add("act", lambda e: e.activation(out=qkT[u], in_=psT2[:, 0:384], func=AF.Copy), reads=[], writes=[f"qkT{u}"], xreads=[f"ps{B_T2}"],
              cost=cA(384))
        qbd = qkT[u][:, 0:256]
        kT = qkT[u][:, 256:384]
        S.add("pe", lambda e: e.matmul(psS[0][:, 0:256], lhsT=kT, rhs=qbd, start=True, stop=True),
              reads=[f"qkT{u}"], writes=[f"ps{B_S0}"], cost=cPE([256]))
        S.add("dve", lambda e: e.tensor_tensor(out=Mm[u], in0=psS[0][:, 0:256], in1=cmask2, op=ALU.mult),
              reads=["cmask2"], writes=[f"Mm{u}"], xreads=[f"ps{B_S0}"], cost=cD(256))

        def omm(e):
            ins = None
            for h in range(2):
                ins = e.matmul(psO[:, h * 64:(h + 1) * 64], lhsT=Mm[u][:, h * 128:(h + 1) * 128], rhs=vb[u][:, h * 64:(h + 1) * 64],
                               start=True, stop=(t == 0))
                if t > 0:
                    ins = e.matmul(psO[:, h * 64:(h + 1) * 64], lhsT=qkT[u][h * 64:(h + 1) * 64, h * 128:(h + 1) * 128],
                                   rhs=Sb[h * 64:(h + 1) * 64, :], start=False, stop=True)
            for h in range(2):
                ins = e.matmul(psO[h * 64:(h + 1) * 64, 128:192], lhsT=kb[u][:, h * 64:(h + 1) * 64], rhs=vb[u][:, h * 64:(h + 1) * 64],
                               start=True, stop=True)
            return ins
        S.add("pe", omm, reads=[f"Mm{u}", f"vb{u}", f"qkT{u}", "Sb", f"kb{u}"], writes=[f"ps{B_O}"], cost=6 * 75)
        o3f = o_sb[u].rearrange("p (h c) -> p h c", h=2)
        o3 = o3f[:, :, 0:64]
        q3f = osq[u].rearrange("p (h c) -> p h c", h=2)
        S.add("dve", lambda e: e.scalar_tensor_tensor(out=St, in0=St, scalar=cvec, in1=psO[:, 128:192], op0=ALU.mult, op1=ALU.add),
              reads=["St", "cvec"], writes=["St"], xreads=[f"ps{B_O}"], cost=cD(64))
        S.add("pool", lambda e: e.tensor_copy(out=Sb, in_=St), reads=["St"], writes=["Sb"], cost=380)
        S.add("dve", lambda e: e.tensor_tensor(out=o3, in0=psO[:, 0:128].rearrange("p (h e) -> p h e", h=2),
                                               in1=sk[u][:, 6:8].unsqueeze(2).broadcast_to([128, 2, 64]), op=ALU.mult),
              reads=[f"sk{u}"], writes=[f"o_sb{u}"], xreads=[f"ps{B_O}"], cost=cD(128))
        for h in range(2):
            S.add("dve", lambda e, h=h: e.scalar_tensor_tensor(out=q3f[:, h, :], in0=o3f[:, h, :], scalar=1.0, in1=o3f[:, h, :],
                                                                op0=ALU.mult, op1=ALU.mult, accum_out=ss2[u][:, h:h + 1]),
                  reads=[f"o_sb{u}"], writes=[f"osq{u}h{h}", f"ss2{u}h{h}"], cost=cD(65, True))
        S.add("pool", lambda e: e.tensor_tensor(out=r2[u], in0=ss2[u], in1=neghalf.broadcast_to([128, 2]), op=ALU.pow),
              reads=[f"ss2{u}h0", f"ss2{u}h1", "small"], writes=[f"r2{u}"], cost=cPow(2))
        S.add("dve", lambda e: e.tensor_tensor(out=otmp[u].rearrange("p (h e) -> p h e", h=2), in0=o3,
                                               in1=r2[u].unsqueeze(2).broadcast_to([128, 2, 64]), op=ALU.mult),
              reads=[f"o_sb{u}", f"r2{u}"], writes=[f"otmp{u}"], cost=cD(128))
        S.add("pool", lambda e: e.tensor_tensor(out=mixr[u], in0=otmp[u], in1=gg[u], op=ALU.mult), reads=[f"otmp{u}", f"gg{u}"],
              writes=[f"mixr{u}"], cost=cP(128))
        S.add("pe", lambda e: e.transpose(psMX[:, 0:128], mixr[u], ident), reads=[f"mixr{u}", "ident"], writes=[f"ps{B_S1}"], cost=75)
        S.add("act", lambda e: e.activation(out=mretR[G % 2][:, (t % 4) * 128:(t % 4 + 1) * 128], in_=psMX[:, 0:128], func=AF.Copy),
              reads=[], writes=[f"mretR{G % 2}"], xreads=[f"ps{B_S1}"], cost=cA(128))
        if t % 4 == 3 and stop_after is None:
            dma("sp", mixsrc[G * 256: G * 256 + 128, :], mretR[G % 2], [f"mretR{G % 2}"], [f"d:ms{G}"], nbytes=128 * 1024, grp=("ms", 2))
        if t % 4 == 3 and stop_after == "B":
            dma("sp", y_d[0:128, G * 512:(G + 1) * 512], mretR[G % 2], [f"mretR{G % 2}"], ["d:y0"], nbytes=128 * 1024)
        S.add("dve", lambda e: e.tensor_tensor(out=tmpd[u].rearrange("p (a d) -> p a d", d=64),
                                               in0=qkraw[u].rearrange("p (a d) -> p a d", d=64),
                                               in1=sk[u][:, 0:4].unsqueeze(2).broadcast_to([128, 4, 64]), op=ALU.mult),
              reads=[f"sk{u}", f"qkraw{u}"], writes=[f"tmpd{u}"], cost=cD(256))
        S.add("pool", lambda e: e.tensor_tensor(out=qkh[u], in0=tmpd[u], in1=gqk, op=ALU.mult), reads=[f"tmpd{u}", "gqk"],
              writes=[f"qkh{u}"], cost=cP(256))

        def tr_qkh(e):
            e.transpose(psT3[:, 0:128], qkh[u][:, 0:128], ident)
            return e.transpose(psT3[:, 128:256], qkh[u][:, 128:256], ident)
        S.add("pe", tr_qkh, reads=[f"qkh{u}", "ident"], writes=[f"ps{B_T3}"], cost=150)
        S.add("act", lambda e: e.activation(out=qdT[:, t * 128:(t + 1) * 128], in_=psT3[:, 0:128], func=AF.Copy),
              reads=[], writes=[f"qdT{t}"], xreads=[f"ps{B_T3}"], cost=cA(128))
        S.add("act", lambda e: e.activation(out=kdT[:, t * 128:(t + 1) * 128], in_=psT3[:, 128:256], func=AF.Copy),
              reads=[], writes=[f"kdT{t}", f"done{t % 8}"], xreads=[f"ps{B_T3}"], cost=cA(128))

    S.tag = "A"
    for t in range(NT):
        S.tile = t
        phaseA_tile(t)
    S.tile = ""
    S.tag = "Wpre"
    wo3 = wo_b.rearrange("p (k n) -> p k n", k=8)
    wg3 = wg_b.rearrange("p (k n) -> p k n", k=8)
    wu3 = wu_b.rearrange("p (k n) -> p k n", k=8)
    wd3 = wd_b.rearrange("p (k n) -> p k n", k=NFC)
    wo_src = wout_d.rearrange("(k p) n -> p k n", p=128)
    wg_src = wg_d.rearrange("(k p) n -> p k n", p=128)
    wu_src = wu_d.rearrange("(k p) n -> p k n", p=128)
    wd_src = wd_d.rearrange("(k p) n -> p k n", p=128)
    if stop_after is None:
        for kc in range(0, 8, 2):
            dma("pool", wo3[:, kc:kc + 2, :], wo_src[:, kc:kc + 2, :], ["d:w_out"], ["wo"], nbytes=128 * 2048 * 4, grp=("w", 4))
        for kc in range(8):
            dma("pool", wg3[:, kc, :], wg_src[:, kc, :], ["d:w_gate"], [f"wg{kc}"], nbytes=128 * DFF * 4, grp=("w", 4))
        WD_LATE = [fc for fc in range(NFC) if WBASE + (NFC - 1 - fc) * 512 < L_END]
        for fc in range(NFC):
            if fc not in WD_LATE:
                dma("pool", wd3[:, NFC - 1 - fc, :], wd_src[:, fc, :], ["d:w_down"], [f"wd{fc}"], nbytes=128 * 1024 * 4, grp=("w", 4))
        for kc in range(5):
            dma("pool", wu3[:, kc, :], wu_src[:, kc, :], ["d:w_up"], [f"wu{kc}"], nbytes=128 * DFF * 4, grp=("w", 4))
    WU_LATE = {4: 5, 10: 6, 15: 7}

    SB = [(0, 1), (2, 3)]
    OB = (4, 5, 6)
    B_TD = 7
    psTD = bank_bf(B_TD)

    def acc_ap(a, c0, c1):
        bk = OB[a // 3]
        base = (a % 3) * 130
        return PS[:, bk * 512 + base + c0: bk * 512 + base + c1]

    def phaseB_group(G):
        njt = 4 * G + 4
        pend = None
        for jt in range(njt):
            buf = jt % 2
            dl = jt - 4 * G
            il0 = max(dl, 0)
            c0 = il0 * 128
            b0, b1 = SB[buf]

            def qk(e, jt=jt, c0=c0, b0=b0, b1=b1):
                e.matmul(PS[:, b0 * 512 + c0: b0 * 512 + 512], lhsT=kdT[0:64, jt * 128:(jt + 1) * 128],
                         rhs=qdT[0:64, G * 512 + c0: G * 512 + 512], start=True, stop=True)
                return e.matmul(PS[:, b1 * 512 + c0: b1 * 512 + 512], lhsT=kdT[64:128, jt * 128:(jt + 1) * 128],
                                rhs=qdT[64:128, G * 512 + c0: G * 512 + 512], start=True, stop=True)
            S.add("pe", qk, reads=[f"kdT{jt}"] + [f"qdT{4 * G + i}" for i in range(il0, 4)], writes=[f"ps{b0}", f"ps{b1}"],
                  cost=cPE([512 - c0]) + 10, prio=0)
            s_in = PS[:, b0 * 512: b0 * 512 + 1024].rearrange("p (m i) -> p m i", m=2)[:, :, c0:512]
            p_out = PT[buf].rearrange("p (m i) -> p m i", m=2)[:, :, c0:512]
            S.add("act", lambda e, s_in=s_in, p_out=p_out: e.activation(out=p_out, in_=s_in, func=AF.Exp, bias=negM, scale=0.125),
                  reads=["small"], writes=[f"PT{buf}"], xreads=[f"ps{b0}", f"ps{b1}"], cost=cA(2 * (512 - c0)))
            if dl >= 0:
                blk = PT[buf].rearrange("p (m i) -> p m i", m=2)[:, :, dl * 128:(dl + 1) * 128]
                S.add("dve", lambda e, blk=blk: e.tensor_tensor(out=blk, in0=blk, in1=cmask.unsqueeze(1).broadcast_to([128, 2, 128]),
                                                               op=ALU.mult), reads=["cmask", f"PT{buf}"], writes=[f"PT{buf}"], cost=cD(256))
            if pend is not None:
                pend()

            def pv_decl(jt=jt, il0=il0, buf=buf):
                def pv(e):
                    ins = None
                    for il in range(il0, 4):
                        for m in range(2):
                            a = il * 2 + m
                            ins = e.matmul(acc_ap(a, 0, 129), lhsT=PT[buf][:, m * 512 + il * 128: m * 512 + (il + 1) * 128],
                                           rhs=vd3[:, jt, 0:129], start=(jt == 0 and a % 3 == 0), stop=(jt == 4 * G + il),
                                           skip_group_check=True)
                    return ins
                S.add("pe", pv, reads=[f"PT{buf}", f"vd{jt}"], writes=[f"ps{b}" for b in OB], cost=cPE([129] * (2 * (4 - il0))))
            pend = pv_decl
        pend()
        O3 = Osb.rearrange("p (a c) -> p a c", c=130)
        for bi, bk in enumerate(OB):
            n = 3 if bi < 2 else 2
            S.add("dve", lambda e, bi=bi, bk=bk, n=n: e.tensor_copy(
                out=Osb[:, bi * 390: bi * 390 + n * 130].rearrange("p (a c) -> p a c", c=130)[:, :, 0:129],
                in_=PS[:, bk * 512: bk * 512 + n * 130].rearrange("p (a c) -> p a c", c=130)[:, :, 0:129]),
                  reads=[], writes=["Osb"], xreads=[f"ps{bk}"], cost=cD(n * 130))
        S.add("dve", lambda e: e.reciprocal(out=rl, in_=O3[:, :, 128]), reads=["Osb"], writes=["rl"], cost=cD(8))
        rl2 = rl.rearrange("p (i m) -> p i m", m=2)
        S.add("dve", lambda e: e.tensor_scalar(out=coef, in0=rl2[:, :, 1], scalar1=neglam, scalar2=None, op0=ALU.mult),
              reads=["rl", "small"], writes=["coef"], cost=cD(4))
        for il in range(4):
            it = 4 * G + il
            S.add("dve", lambda e, il=il: e.tensor_scalar(out=dtmp, in0=O3[:, 2 * il, 0:128], scalar1=rl[:, 2 * il:2 * il + 1],
                                                         scalar2=None, op0=ALU.mult), reads=["Osb", "rl"], writes=["dtmp"], cost=cD(128))
            S.add("dve", lambda e, il=il: e.scalar_tensor_tensor(out=dtmp, in0=O3[:, 2 * il + 1, 0:128], scalar=coef[:, il:il + 1],
                                                                in1=dtmp, op0=ALU.mult, op1=ALU.add),
                  reads=["Osb", "coef", "dtmp"], writes=["dtmp"], cost=cD(128))
            S.add("dve", lambda e: e.scalar_tensor_tensor(out=dsq, in0=dtmp, scalar=1.0, in1=dtmp, op0=ALU.mult, op1=ALU.mult,
                                                          accum_out=ssd), reads=["dtmp"], writes=["dsq", "ssd"], cost=cD(128, True))
            S.add("dve", lambda e: e.tensor_scalar(out=rd, in0=ssd, scalar1=1.0 / 128, scalar2=EPS, op0=ALU.mult, op1=ALU.add),
                  reads=["ssd"], writes=["rd"], cost=cD(1))
            S.add("pool", lambda e: e.tensor_tensor(out=rd, in0=rd, in1=neghalf, op=ALU.pow), reads=["small", "rd"], writes=["rd"], cost=cPow(1))
            S.add("dve", lambda e: e.scalar_tensor_tensor(out=difn, in0=dtmp, scalar=rd, in1=gsub, op0=ALU.mult, op1=ALU.mult),
                  reads=["dtmp", "rd", "gsub"], writes=["difn"], cost=cD(128))
            S.add("pe", lambda e: e.transpose(psTD[:, 0:128], difn, ident), reads=["difn", "ident"], writes=[f"ps{B_TD}"], cost=75)
            S.add("dve", lambda e, il=il: e.tensor_copy(out=mdifR[G % 2][:, il * 128:(il + 1) * 128], in_=psTD[:, 0:128]),
                  reads=[], writes=[f"mdifR{G % 2}"], xreads=[f"ps{B_TD}"], cost=cD(128, fast=True))

    def exchange_group(G):
        dma("sp", mixsrc[G * 256 + 128: G * 256 + 256, :], mdifR[G % 2], [f"mdifR{G % 2}"], [f"d:ms{G}"], nbytes=128 * 1024,
            grp=("ms2", 2))
        S.add("pool", lambda e: e.collective_compute("AllGather", ALU.bypass, replica_groups=[[0, 1, 2, 3], [4, 5, 6, 7]],
                                                     ins=[mixsrc_t.ap()[G * 256:(G + 1) * 256, :].opt()],
                                                     outs=[gath_t.ap()[G * 1024:(G + 1) * 1024, :].opt()]),
              reads=[f"d:ms{G}"], writes=[f"d:gath{G}"], cc=True, cost=1000, lat=30000)

    for G in range(NT // 4):
        S.tag = "B"
        phaseB_group(G)
        if stop_after is None:
            S.tag = "X"
            exchange_group(G)
            if G in WU_LATE:
                kc = WU_LATE[G]
                dma("pool", wu3[:, kc, :], wu_src[:, kc, :], ["d:w_up"], [f"wu{kc}"], nbytes=128 * DFF * 4, grp=("w", 4))
        else:
            dma("sp", y_d[128:256, G * 512:(G + 1) * 512], mdifR[G % 2], [f"mdifR{G % 2}"], ["d:y1"], nbytes=128 * 1024)

    if stop_after == "B":
        S.add("sp", lambda e: e.wait_ge(S.sem["sp"], 0), reads=["d:y0", "d:y1"], writes=[], dma=False, cost=50)
        S.emit(reorder)
        return nc

    S.tag = "Wlate"
    dma("sp", g2col, g2_d, ["d:const"], ["g2col"])
    for fc in WD_LATE:
        dma("pool", wd3[:, NFC - 1 - fc, :], wd_src[:, fc, :], ["d:w_down"], [f"wd{fc}"], nbytes=128 * 1024 * 4, grp=("w", 4))
    NG = TSH // 256
    rank_cache = {}
    PB_DN = (0, 1)
    PB_T = 2
    PB_G = (3, 4)
    PB_U = (5, 6)
    PB_OP = 7
    psTC = bank_bf(PB_T)
    ff3 = ffT.rearrange("p (f t) -> p f t", f=NFC)

    def stageX(g):
        s = g % 2
        X1 = x1[s].rearrange("p (t n) -> p t n", t=2)
        dma("sp", X1, xres_d[g * 256:(g + 1) * 256, :].rearrange("(t p) n -> p t n", p=128), ["d:xres"], [f"x1_{s}"], nbytes=1 << 20,
            grp=("x1", 2))
        mg3 = mixg.rearrange("p (c t) -> p c t", c=8)

        def ld_mix(e):
            if "r" not in rank_cache:
                rank_cache["r"] = e.partition_id() % 4
            src_ap = gath[bass.ds(rank_cache["r"] * 4096 + (g // 2) * 1024, 1024), (g % 2) * 256:(g % 2) * 256 + 256]
            return e.dma_start(out=mg3, in_=src_ap.rearrange("(c p) t -> p c t", p=128))
        S.add("pool", ld_mix, reads=[f"d:gath{r * 4 + g // 2}" for r in range(4)], writes=["mixg"], dma=True, cost=900, lat=6000,
              semgrp=("mixg", 2))
        h2T3 = h2T[s].rearrange("p (k t) -> p k t", k=8)
        for tt in range(2):
            for nh in range(2):
                def op_mm(e, tt=tt, nh=nh):
                    ins = None
                    for c in range(8):
                        ins = e.matmul(bank(PB_OP), lhsT=mg3[:, c, tt * 128:(tt + 1) * 128], rhs=wo3[:, c, nh * 512:(nh + 1) * 512],
                                       start=(c == 0), stop=(c == 7))
                    return ins
                S.add("pe", op_mm, reads=["mixg", "wo"], writes=[f"ps{PB_OP}"], cost=cPE([512] * 8))
                S.add("dve", lambda e, tt=tt, nh=nh: e.tensor_tensor(out=X1[:, tt, nh * 512:(nh + 1) * 512], in0=bank(PB_OP),
                                                                    in1=X1[:, tt, nh * 512:(nh + 1) * 512], op=ALU.add),
                      reads=[f"x1_{s}"], writes=[f"x1_{s}"], xreads=[f"ps{PB_OP}"], cost=cD(512))
        for tt in range(2):
            S.add("act", lambda e, tt=tt: e.activation(out=mixg[:, 0:DM], in_=X1[:, tt, :], func=AF.Square, accum_out=ssc[:, tt:tt + 1]),
                  reads=[f"x1_{s}"], writes=["mixg", "ssc"], cost=cA(1024, True))
            S.add("dve", lambda e, tt=tt: e.tensor_scalar(out=rsc[:, tt:tt + 1], in0=ssc[:, tt:tt + 1], scalar1=1.0 / DM, scalar2=EPS,
                                                         op0=ALU.mult, op1=ALU.add), reads=["ssc"], writes=["rsc"], cost=cD(1))
            S.add("pool", lambda e, tt=tt: e.tensor_tensor(out=rsc[:, tt:tt + 1], in0=rsc[:, tt:tt + 1], in1=neghalf, op=ALU.pow),
                  reads=["small", "rsc"], writes=["rsc"], cost=cPow(1))
            S.add("act", lambda e, tt=tt: e.activation(out=h2, in_=X1[:, tt, :], func=AF.Copy, scale=rsc[:, tt:tt + 1]),
                  reads=[f"x1_{s}", "rsc"], writes=["h2"], cost=cA(1024))

            def tr_h(e):
                ins = None
                for kc in range(8):
                    ins = e.transpose(psTC[:, kc * 128:(kc + 1) * 128], h2[:, kc * 128:(kc + 1) * 128], ident)
                return ins
            S.add("pe", tr_h, reads=["h2", "ident"], writes=[f"ps{PB_T}"], cost=8 * 75)
            S.add("dve", lambda e, tt=tt: e.tensor_tensor(out=h2T3[:, :, tt * 128:(tt + 1) * 128],
                                                         in0=psTC.rearrange("p (k t) -> p k t", k=8),
                                                         in1=g2col.unsqueeze(2).broadcast_to([128, 8, 128]), op=ALU.mult),
                  reads=["g2col"], writes=[f"h2T{s}"], xreads=[f"ps{PB_T}"], cost=cD(1024))

    def stageF(g):
        s = g % 2
        X1 = x1[s].rearrange("p (t n) -> p t n", t=2)
        h2T3 = h2T[s].rearrange("p (k t) -> p k t", k=8)
        for fc in range(NFC):
            pg = PB_G[fc % 2]
            pu = PB_U[fc % 2]
            sgc = sg[fc % 2]

            def gu(e, fc=fc, pg=pg, pu=pu):
                ins = None
                for c in range(8):
                    ins = e.matmul(bank(pg, 0, 256), lhsT=wg3[:, c, fc * 128:(fc + 1) * 128], rhs=h2T3[:, c, :],
                                   start=(c == 0), stop=(c == 7))
                for c in range(8):
                    ins = e.matmul(bank(pu, 0, 256), lhsT=wu3[:, c, fc * 128:(fc + 1) * 128], rhs=h2T3[:, c, :],
                                   start=(c == 0), stop=(c == 7))
                return ins
            S.add("pe", gu, reads=WG_ALL + WU_ALL + [f"h2T{s}"], writes=[f"ps{pg}", f"ps{pu}"], cost=cPE([256] * 16))
            S.add("act", lambda e, pg=pg, sgc=sgc: e.activation(out=sgc, in_=bank(pg, 0, 256), func=AF.Silu),
                  reads=[], writes=[f"sg{fc % 2}"], xreads=[f"ps{pg}"], cost=cA(256))
            S.add("dve", lambda e, fc=fc, pu=pu, sgc=sgc: e.tensor_tensor(out=ff3[:, fc, :], in0=bank(pu, 0, 256), in1=sgc, op=ALU.mult),
                  reads=[f"sg{fc % 2}"], writes=["ffT"], xreads=[f"ps{pu}"], cost=cD(256))
        for tt in range(2):
            for nh in range(2):
                pb = PB_DN[nh]

                def down(e, tt=tt, nh=nh, pb=pb):
                    ins = None
                    for fc in range(NFC):
                        ins = e.matmul(bank(pb), lhsT=ff3[:, fc, tt * 128:(tt + 1) * 128], rhs=wd3[:, NFC - 1 - fc, nh * 512:(nh + 1) * 512],
                                       start=(fc == 0), stop=(fc == NFC - 1))
                    return ins
                S.add("pe", down, reads=["ffT"] + WD_ALL, writes=[f"ps{pb}"], cost=cPE([512] * NFC))
                S.add("dve", lambda e, tt=tt, nh=nh, pb=pb: e.tensor_tensor(out=X1[:, tt, nh * 512:(nh + 1) * 512], in0=bank(pb),
                                                                           in1=X1[:, tt, nh * 512:(nh + 1) * 512], op=ALU.add),
                      reads=[f"x1_{s}"], writes=[f"x1_{s}"], xreads=[f"ps{pb}"], cost=cD(512))
        dma("sp", y_d[g * 256:(g + 1) * 256, :].rearrange("(t p) n -> p t n", p=128), X1, [f"x1_{s}"], [f"d:y{g}"], nbytes=1 << 20,
            grp=("y", 2))

    S.tag = "C"
    stageX(0)
    for g in range(NG):
        if g + 1 < NG:
            stageX(g + 1)
        stageF(g)
    S.add("sp", lambda e: e.wait_ge(S.sem["sp"], 0), reads=[f"d:y{g}" for g in range(8)], writes=[], dma=False, cost=50)
    S.emit(reorder)
    print("sched estimate (us):", S.est_ns / 1e3 if hasattr(S, "est_ns") else None)
    return nc


def _const_tables():
    pos = np.arange(SEQ, dtype=np.float32)
    freqs = (1.0 / (10000.0 ** (np.arange(0, 64, 2, dtype=np.float32) / np.float32(64)))).astype(np.float32)
    ang = (pos[:, None] * freqs[None, :]).astype(np.float32)
    cos = np.cos(ang.astype(np.float64)).astype(np.float32)
    sin = np.sin(ang.astype(np.float64)).astype(np.float32)
    cos_t = cos.reshape(NT, 128, 32).transpose(1, 0, 2).reshape(128, NT * 32)
    ns = np.stack([-sin, sin], axis=1)
    sin_t = ns.reshape(NT, 128, 2, 32).transpose(1, 0, 2, 3).reshape(128, NT * 64)
    ident = np.eye(128, dtype=np.float32).astype(ml_dtypes.bfloat16)
    j = np.arange(128)
    cmask = (j[:, None] <= j[None, :]).astype(np.float32).astype(ml_dtypes.bfloat16)
    return np.ascontiguousarray(cos_t), np.ascontiguousarray(sin_t), ident, cmask


def _decay_tables(hg):
    idx = np.arange(128, dtype=np.float64)
    j = np.arange(128)
    causal = (j[:, None] <= j[None, :]).astype(np.float64)
    kod = np.zeros((128, 4), np.float32)
    cvec = np.zeros((128, 1), np.float32)
    cm2 = np.zeros((128, 256), np.float32)
    for hl in range(2):
        h = 2 * hg + hl
        lg = math.log(1.0 - 2.0 ** (-5.0 - h))
        c = math.exp(lg * 128.0)
        kod[:, hl] = np.exp(-lg * (idx + 1.0)) * (64 ** -0.5) * c
        kod[:, 2 + hl] = np.exp(lg * (idx + 1.0))
        cvec[hl * 64:(hl + 1) * 64, 0] = c
        cm2[:, hl * 128:(hl + 1) * 128] = causal / c
    return kod, cvec, cm2


_NC_CACHE = {}


def _prep_inputs(inputs):
    f = lambda a: np.ascontiguousarray(np.asarray(a, dtype=np.float32))
    x = f(inputs["x"])
    w_in = f(inputs["w_in"])[0]
    w_out = f(inputs["w_out"])[0]
    w_gate = f(inputs["w_gate"])[0]
    w_up = f(inputs["w_up"])[0]
    w_down = f(inputs["w_down"])[0]
    rep = lambda v, n=128: np.ascontiguousarray(np.broadcast_to(np.asarray(v, np.float32).reshape(1, -1), (n, np.asarray(v).size)))
    g1col = np.ascontiguousarray(np.asarray(inputs["norm1_g"][0], np.float32).reshape(8, 128).T)
    g2col = np.ascontiguousarray(np.asarray(inputs["norm2_g"][0], np.float32).reshape(8, 128).T)
    gq = np.asarray(inputs["diff_q_norm_g"][0], np.float32)
    gk = np.asarray(inputs["diff_k_norm_g"][0], np.float32)
    gqk = rep(np.concatenate([gq, gq, gk, gk]))
    gsub = rep(inputs["diff_subln_g"][0])
    lam4 = rep(np.concatenate([np.asarray(inputs[k][0], np.float32) for k in ("lambda_q1", "lambda_q2", "lambda_k1", "lambda_k2")]))
    cos_t, sin_t, ident, cmask = _const_tables()
    in_maps = []
    for c in range(NCORES):
        b, hg = divmod(c, 4)
        cols = []
        for base in (0, 512, 1024, 1536):
            cols += list(range(base + hg * 128, base + hg * 128 + 128))
        for base in (2048, 2560, 3072):
            cols += list(range(base + hg * 128, base + hg * 128 + 128))
        rows = []
        for r in range(4):
            rows += list(range(r * 128, r * 128 + 128)) + list(range(512 + r * 128, 512 + r * 128 + 128))
        kod, cvec, cm2 = _decay_tables(hg)
        gret = rep(np.asarray(inputs["ret_norm_g"][0], np.float32)[2 * hg:2 * hg + 2].reshape(-1))
        in_maps.append({
            "x": x[b],
            "xres": np.ascontiguousarray(x[b, hg * TSH:(hg + 1) * TSH]),
            "w_in": np.ascontiguousarray(w_in[:, cols]),
            "w_out": np.ascontiguousarray(w_out[rows, :]),
            "w_gate": w_gate, "w_up": w_up, "w_down": w_down,
            "g1col": g1col, "g2col": g2col,
            "cos_t": cos_t, "sin_t": sin_t,
            "kod": kod, "cvec": cvec, "cmask2": cm2,
            "gret": gret, "gqk": gqk, "gsub": gsub, "lam4": lam4,
            "ident": ident, "cmask": cmask,
        })
    return in_maps


def kernel(**inputs):
    in_maps = _prep_inputs(inputs)
    if "nc" not in _NC_CACHE:
        _NC_CACHE["nc"] = build_nc()
    nc = _NC_CACHE["nc"]
    res = run_bass_kernel_spmd(nc, in_maps, core_ids=list(range(NCORES)))
    out = np.empty((2, SEQ, DM), np.float32)
    for c in range(NCORES):
        b, hg = divmod(c, 4)
        out[b, hg * TSH:(hg + 1) * TSH] = res.results[c]["y"]
    return out
```

```python
import math
import numpy as np
import ml_dtypes
import concourse.bass as bass
import concourse.mybir as mybir
from concourse.bass_utils import run_bass_kernel_spmd

F32 = mybir.dt.float32
BF16 = mybir.dt.bfloat16
AF = mybir.ActivationFunctionType
ALU = mybir.AluOpType
AX = mybir.AxisListType

SEQ = 8192
DM = 1024
NT = SEQ // 128
TSH = 2048
DFF = 2816
NFC = DFF // 128
EPS = 1e-6
import os
RUNAHEAD = int(os.environ.get("K_RUNAHEAD", "8"))
NSLOT = int(os.environ.get("K_NS", "4"))
XCAST_DMA = int(os.environ.get("K_XCAST", "0"))
PRIO_EVAC = int(os.environ.get("K_PRIO", "1"))
SLACK = float(os.environ.get("K_SLACK", "0"))
LAMBDA_INIT = 0.8 - 0.6 * math.exp(-0.3 * 0)
NCORES = 8


class _Op:
    __slots__ = ("stream", "fn", "deps", "order", "dma", "sem", "inc", "sig", "needs", "idx", "busy", "lat", "tag", "t0", "t1", "wr", "rdk", "prio")


def cA(n, acc=False):
    return 300 + 0.8 * n + (226 if acc else 0)


def cD(n, acc=False, fast=False):
    if n <= 8:
        return 280
    return (130 if fast else 200) + (0.55 if fast else 1.0) * n + (85 if acc else 0)


def cP(n):
    return 224 + 2.3 * n


def cPow(n):
    return 420 + 140 * n


def cPE(cols):
    return sum(max(c, 128) * 0.52 + 6 for c in cols)


class Sched:
    STREAMS = ("pe", "act", "dve", "pool", "sp")

    def __init__(self, nc, n_dma_sems=6):
        self.nc = nc
        self.ops = []
        self.keys = {}
        self.overl = {}
        self.lastw = {}
        self.rd_c = {}
        self.rd_d = {}
        self.sem = {s: nc.alloc_semaphore("sem_" + s) for s in self.STREAMS}
        self.dsems = {}
        self.drr = {}
        self.sem_last = {}
        self.ccsems = [nc.alloc_semaphore(f"ccsem{i}") for i in range(4)]
        self.ccrr = 0

    def reg(self, key, space, lo, hi):
        assert key not in self.keys, key
        ov = []
        for k, (sp, l, h) in self.keys.items():
            if sp == space and l < hi and lo < h:
                ov.append(k)
                self.overl[k].append(key)
        self.keys[key] = (space, lo, hi)
        self.overl[key] = ov + [key]

    def add(self, stream, fn, reads=(), writes=(), xreads=(), dma=False, cc=False, cost=300.0, lat=None, semgrp=("misc", 3)):
        op = _Op()
        op.wr = tuple(writes) + tuple(xreads)
        op.prio = 0 if (len(xreads) > 0 and PRIO_EVAC) else 1
        op.rdk = tuple(reads)
        op.stream, op.fn, op.dma = stream, fn, (dma or cc)
        op.deps, op.order, op.needs, op.sig = {}, {}, False, None
        op.idx = len(self.ops)
        op.tag = getattr(self, "tag", "") + str(getattr(self, "tile", ""))
        op.busy = float(cost)
        op.lat = float(lat) if lat is not None else float(cost)

        def dep(o, sync_same=True):
            if o is None or o is op:
                return
            if (not op.dma) and (not o.dma) and o.stream == stream:
                if stream == "pe" or not sync_same:
                    op.order[o.idx] = o
                    return
            op.deps[o.idx] = o

        for k in reads:
            for kk in self.overl[k]:
                dep(self.lastw.get(kk))
        for k in xreads:
            for kk in self.overl[k]:
                dep(self.lastw.get(kk))
                for r in self.rd_c.get(kk, ()):
                    dep(r, sync_same=False)
                for r in self.rd_d.get(kk, ()):
                    dep(r)
        for k in writes:
            for kk in self.overl[k]:
                dep(self.lastw.get(kk))
                for r in self.rd_c.get(kk, ()):
                    dep(r)
                for r in self.rd_d.get(kk, ()):
                    dep(r)
        if cc:
            op.sem, op.inc = self.ccsems[self.ccrr % len(self.ccsems)], 1
            self.ccrr += 1
            dep(self.sem_last.get(id(op.sem)))
            self.sem_last[id(op.sem)] = op
        elif dma:
            gk = (stream, semgrp[0])
            if gk not in self.dsems:
                self.dsems[gk] = [self.nc.alloc_semaphore(f"dsem_{stream}_{semgrp[0]}{i}") for i in range(semgrp[1])]
                self.drr[gk] = 0
            sems = self.dsems[gk]
            op.sem, op.inc = sems[self.drr[gk] % len(sems)], 16
            self.drr[gk] += 1
            dep(self.sem_last.get(id(op.sem)))
            self.sem_last[id(op.sem)] = op
        else:
            op.sem, op.inc = self.sem[stream], 1
        for k in tuple(reads) + tuple(xreads):
            if op.dma:
                self.rd_d.setdefault(k, []).append(op)
            else:
                self.rd_c.setdefault(k, []).append(op)
        for k in writes:
            self.lastw[k] = op
            self.rd_c[k] = []
            self.rd_d[k] = []
        self.ops.append(op)
        return op

    def schedule(self, reorder=True):
        ops = self.ops
        n = len(ops)
        if not reorder:
            return {s: [o for o in ops if o.stream == s] for s in self.STREAMS}
        HOP = 150.0
        succ = [[] for _ in range(n)]
        indeg = [0] * n
        for o in ops:
            for p in o.deps.values():
                succ[p.idx].append((o.idx, HOP))
                indeg[o.idx] += 1
            for p in o.order.values():
                if p.idx not in o.deps:
                    succ[p.idx].append((o.idx, 0.0))
                    indeg[o.idx] += 1
        dready = [0.0] * n
        ready = {s: [] for s in self.STREAMS}
        for o in ops:
            if indeg[o.idx] == 0:
                ready[o.stream].append(o.idx)
        free = {s: 0.0 for s in self.STREAMS}
        out = {s: [] for s in self.STREAMS}
        done = 0
        tend = 0.0
        while done < n:
            best = None
            for s in self.STREAMS:
                r = ready[s]
                if not r:
                    continue
                f = free[s]
                bi = None
                for i in r:
                    st = dready[i] if dready[i] > f else f
                    key = (st if st > f + SLACK else f, ops[i].prio, i)
                    if bi is None or key < bi:
                        bi = key
                if best is None or bi < best[0]:
                    best = (bi, s)
            (_, _, i), s = best
            st = dready[i] if dready[i] > free[s] else free[s]
            ready[s].remove(i)
            o = ops[i]
            free[s] = st + o.busy
            fin = st + o.lat
            o.t0, o.t1 = st, fin
            tend = max(tend, fin)
            out[s].append(o)
            done += 1
            for j, hop in succ[i]:
                t = fin + hop
                if t > dready[j]:
                    dready[j] = t
                indeg[j] -= 1
                if indeg[j] == 0:
                    ready[ops[j].stream].append(j)
        self.est_ns = tend
        if getattr(self, "verbose", False):
            tags = []
            base = lambda s: s.rstrip("0123456789")
            for o in ops:
                if base(o.tag) not in tags:
                    tags.append(base(o.tag))
            for tg in tags:
                sel = [o for o in ops if base(o.tag) == tg]
                b = {s: sum(o.busy for o in sel if o.stream == s) for s in self.STREAMS}
                print(f"  [{tg:8s}] t=[{min(o.t0 for o in sel) / 1e3:8.1f},{max(o.t1 for o in sel) / 1e3:8.1f}] us  busy(us): " +
                      " ".join(f"{s}={b[s] / 1e3:7.1f}" for s in self.STREAMS))
        return out

    def emit(self, reorder=True):
        nc = self.nc
        per = self.schedule(reorder)
        cnt = {}
        for op in self.ops:
            if op.dma:
                c = cnt.get(id(op.sem), 0) + op.inc
                cnt[id(op.sem)] = c
                op.sig = c
        pos = {}
        for s in self.STREAMS:
            for i, op in enumerate(per[s]):
                pos[op.idx] = i
        for op in self.ops:
            latest = {}
            for d in op.deps.values():
                if d.dma:
                    continue
                if d.stream not in latest or pos[d.idx] > pos[latest[d.stream].idx]:
                    latest[d.stream] = d
            for d in latest.values():
                d.needs = True
            op.deps = {d.idx: d for d in op.deps.values() if d.dma or latest[d.stream] is d}
        for s in self.STREAMS:
            c = 0
            for op in per[s]:
                if (not op.dma) and op.needs:
                    c += 1
                    op.sig = c

        if reorder:
            order = sorted(self.ops, key=lambda o: (o.t0, o.idx))
        else:
            order = list(self.ops)
        cur = {s: {} for s in self.STREAMS}
        clk = {}
        waits = {}
        for op in order:
            c = cur[op.stream]
            need = {}
            for d in op.deps.values():
                k = id(d.sem)
                if d.sig > need.get(k, (0, None, None))[0]:
                    need[k] = (d.sig, d.sem, d)
            wl = []
            for k, (v, sem, d) in sorted(need.items(), key=lambda kv: -kv[1][2].t1 if reorder else 0):
                if c.get(k, 0) >= v:
                    continue
                wl.append((sem, v))
                c[k] = v
                for kk, vv in clk[d.idx].items():
                    if vv > c.get(kk, 0):
                        c[kk] = vv
            waits[op.idx] = wl
            mine = dict(c)
            if op.sig is not None:
                mine[id(op.sem)] = max(mine.get(id(op.sem), 0), op.sig)
                if not op.dma:
                    c[id(op.sem)] = max(c.get(id(op.sem), 0), 0)
            clk[op.idx] = mine

        def run(eng, ops):
            for op in ops:
                for sem, v in waits[op.idx]:
                    eng.wait_ge(sem, v)
                ins = op.fn(eng)
                if op.sig is not None:
                    ins.then_inc(op.sem, op.inc)

        with nc.Block() as block:
            @block.tensor
            def _(e):
                run(e, per["pe"])

            @block.scalar
            def _(e):
                run(e, per["act"])

            @block.vector
            def _(e):
                run(e, per["dve"])

            @block.gpsimd
            def _(e):
                run(e, per["pool"])

            @block.sync
            def _(e):
                run(e, per["sp"])


class Arena:
    def __init__(self, sched, ap, total):
        self.s, self.ap, self.total, self.off = sched, ap, total, 0

    def _take(self, words, key):
        lo = self.off
        self.off += words
        assert self.off <= self.total, (key, self.off, self.total)
        if key is not None:
            self.s.reg(key, "sb", lo, lo + words)
        return self.ap[:, lo:lo + words]

    def f32(self, n, key):
        return self._take(n, key)

    def bf(self, n, key):
        assert n % 2 == 0
        return self._take(n // 2, key).bitcast(BF16)

    def sub(self, key, parent_lo_words, words):
        self.s.reg(key, "sb", parent_lo_words, parent_lo_words + words)


def build_nc(stop_after=None, reorder=True):
    nc = bass.Bass("TRN2", target_bir_lowering=False)
    S = Sched(nc)
    S.verbose = True
    S.tag = "setup"

    def din(name, shape, dt=F32):
        return nc.dram_tensor(name, list(shape), dt, kind="ExternalInput").ap()

    x_d = din("x", [SEQ, DM])
    xres_d = din("xres", [TSH, DM])
    win_d = din("w_in", [DM, 896])
    wout_d = din("w_out", [DM, DM])
    wg_d = din("w_gate", [DM, DFF])
    wu_d = din("w_up", [DM, DFF])
    wd_d = din("w_down", [DFF, DM])
    g1_d = din("g1col", [128, 8])
    g2_d = din("g2col", [128, 8])
    cos_d = din("cos_t", [128, NT * 32])
    sin_d = din("sin_t", [128, NT * 64])
    kod_d = din("kod", [128, 4])
    cvec_d = din("cvec", [128, 1])
    cm2_d = din("cmask2", [128, 256])
    gret_d = din("gret", [128, 128])
    gqk_d = din("gqk", [128, 256])
    gsub_d = din("gsub", [128, 128])
    lam_d = din("lam4", [128, 256])
    ident_d = din("ident", [128, 128], BF16)
    cmask_d = din("cmask", [128, 128], BF16)
    if stop_after is None:
        y_d = nc.dram_tensor("y", [TSH, DM], F32, kind="ExternalOutput").ap()
    else:
        y_d = nc.dram_tensor("y", [256, SEQ], BF16, kind="ExternalOutput").ap()
    mixsrc_t = nc.dram_tensor("mixsrc", [16 * 256, 512], BF16)
    gath_t = nc.dram_tensor("gath", [16 * 1024, 512], BF16)
    mixsrc = mixsrc_t.ap()
    gath = gath_t.ap()
    for k in ("x", "xres", "w_in", "w_out", "w_gate", "w_up", "w_down", "const"):
        S.reg("d:" + k, "dram_" + k, 0, 1)
    for G in range(16):
        S.reg(f"d:ms{G}", "dram_ms", G, G + 1)
        S.reg(f"d:gath{G}", "dram_gath", G, G + 1)
    for g in range(8):
        S.reg(f"d:y{g}", "dram_y", g, g + 1)

    TOTAL = 49152
    arena_t = nc.alloc_sbuf_tensor("arena", [128, TOTAL], F32)
    A = Arena(S, arena_t.ap(), TOTAL)
    ps_t = nc.alloc_psum_tensor("psum_all", [128, 4096], F32)
    PS = ps_t.ap()
    for b in range(8):
        S.reg(f"ps{b}", "ps", b, b + 1)
        S.reg(f"done{b}", "marker", b, b + 1)

    def bank(b, c0=0, c1=512):
        return PS[:, b * 512 + c0: b * 512 + c1]

    def bank_bf(b):
        return PS[:, b * 512:(b + 1) * 512].bitcast(BF16)

    def dma(stream, out, in_, reads, writes, nbytes=65536, grp=("misc", 3)):
        if stream == "sp":
            return S.add(stream, lambda e: e.dma_start(out=out, in_=in_), reads=reads, writes=writes, dma=True,
                         cost=70, lat=2000 + nbytes / 150.0, semgrp=grp)
        return S.add(stream, lambda e: e.dma_start(out=out, in_=in_), reads=reads, writes=writes, dma=True,
                     cost=900, lat=2500 + nbytes / 100.0, semgrp=grp)

    ident = A.bf(128, "ident")
    cmask = A.bf(128, "cmask")
    small = A.f32(64, "small")
    neghalf = small[:, 0:1]
    lam_e = small[:, 1:3]
    neglam = small[:, 3:4]
    negM = small[:, 4:5]
    gmax = small[:, 5:7]
    lsum = small[:, 7:9]
    PERSIST_END = A.off

    NS = NSLOT
    gsub = A.f32(128, "gsub")
    lam4 = A.f32(256, "lam4")
    lamtmp = A.f32(256, "lamtmp")
    kdT_lo = A.off
    kdT = A.bf(SEQ, None)
    vd_lo = A.off
    vd = A.bf(NT * 130, None)
    vd3 = vd.rearrange("p (t c) -> p t c", c=130)
    for t in range(NT):
        A.sub(f"kdT{t}", kdT_lo + t * 64, 64)
        A.sub(f"vd{t}", vd_lo + t * 65, 65)
    PT = [A.bf(1024, f"PT{i}") for i in range(2)]
    Osb = A.f32(8 * 130, "Osb")
    rl = A.f32(8, "rl")
    coef = A.f32(4, "coef")
    dtmp = A.f32(128, "dtmp")
    dsq = A.f32(128, "dsq")
    ssd = A.f32(1, "ssd")
    rd = A.f32(1, "rd")
    difn = A.bf(128, "difn")
    mdifR = [A.bf(512, f"mdifR{i}") for i in range(2)]
    mretR = [A.bf(512, f"mretR{i}") for i in range(2)]
    L_END = A.off
    win_b = A.bf(8 * 896, "win")
    win3 = win_b.rearrange("p (k n) -> p k n", k=8)
    g1col = A.f32(8, "g1col")
    NX = 3
    xt = [A.f32(DM + 1, f"xt{i}") for i in range(NX)]
    junk = A.bf(DM + 2, "junk")
    NXB = 3
    xb = [A.bf(DM, f"xb{i}") for i in range(NXB)]
    xT = [A.bf(DM, f"xT{i}") for i in range(2)]
    cos_t = A.f32(NT * 32, "cos_t")
    sin_t = A.f32(NT * 64, "sin_t")
    kod = A.f32(4, "kod")
    cvec = A.f32(1, "cvec")
    cmask2 = A.f32(256, "cmask2")
    gret = A.f32(128, "gret")
    gqk = A.f32(256, "gqk")
    St = A.f32(64, "St")
    Sb = A.bf(64, "Sb")

    def two(fn, n, name):
        return [fn(n, f"{name}{i}") for i in range(NS)]
    ss = two(A.f32, 1, "ss")
    rstd = two(A.f32, 1, "rstd")
    nrm = two(A.f32, 6, "nrm")
    comb = two(A.f32, 8, "comb")
    sk = two(A.f32, 8, "sk")
    t1 = two(A.f32, 256, "t1")
    t2 = []
    for i in range(NS):
        lo_ = A.off
        t2.append(A.f32(256, None))
        A.sub(f"t2a{i}", lo_, 128)
        A.sub(f"t2b{i}", lo_ + 128, 128)
    rk = two(A.f32, 128, "rk")
    qb = two(A.bf, 256, "qb")
    kb = two(A.bf, 128, "kb")
    vb = two(A.bf, 128, "vb")
    gate = two(A.f32, 128, "gate")
    gg = two(A.f32, 128, "gg")
    qkT = two(A.bf, 384, "qkT")
    Mm = two(A.bf, 256, "Mm")
    o_sb = two(A.f32, 130, "o_sb")
    osq, ss2 = [], []
    for i in range(NS):
        lo_ = A.off
        osq.append(A.f32(130, None))
        A.sub(f"osq{i}h0", lo_, 65)
        A.sub(f"osq{i}h1", lo_ + 65, 65)
        lo_ = A.off
        ss2.append(A.f32(2, None))
        A.sub(f"ss2{i}h0", lo_, 1)
        A.sub(f"ss2{i}h1", lo_ + 1, 1)
    r2 = two(A.f32, 2, "r2")
    otmp = two(A.f32, 128, "otmp")
    mixr = two(A.bf, 128, "mixr")
    sqd = two(A.f32, 256, "sqd")
    qkraw = two(A.f32, 256, "qkraw")
    msd = two(A.f32, 4, "msd")
    tmpd = two(A.f32, 256, "tmpd")
    qkh = two(A.bf, 256, "qkh")
    WA_END = A.off
    assert WA_END <= TOTAL - 7168, WA_END
    A.off = TOTAL - 7168
    winf_lo = A.off
    winf = A.f32(8 * 896, None)
    for kc in range(8):
        A.sub(f"winf{kc}", winf_lo + kc * 896, 896)
    A.off = TOTAL - 4096
    qdT_lo = A.off
    qdT = A.bf(SEQ, None)
    for t in range(NT):
        A.sub(f"qdT{t}", qdT_lo + t * 64, 64)
    A.off = PERSIST_END
    g2col = A.f32(8, "g2col")
    x1 = [A.f32(2 * DM, f"x1_{i}") for i in range(2)]
    mixg = A.bf(8 * 256, "mixg")
    h2 = A.bf(DM, "h2")
    h2T = [A.bf(8 * 256, f"h2T{i}") for i in range(2)]
    ffT = A.bf(NFC * 256, "ffT")
    sg = [A.f32(256, f"sg{i}") for i in range(2)]
    ssc = A.f32(2, "ssc")
    rsc = A.f32(2, "rsc")
    CA_END = A.off
    WBASE = TOTAL - (NFC * DM + 8 * DM + 2 * 8 * DFF) // 2
    assert CA_END <= WBASE, (CA_END, WBASE)
    A.off = WBASE
    wd_lo = A.off
    wd_b = A.bf(NFC * DM, None)
    for fc in range(NFC):
        A.sub(f"wd{fc}", wd_lo + (NFC - 1 - fc) * 512, 512)
    wo_b = A.bf(8 * DM, "wo")
    wg_lo = A.off
    wg_b = A.bf(8 * DFF, None)
    wu_lo = A.off
    wu_b = A.bf(8 * DFF, None)
    for kc in range(8):
        A.sub(f"wg{kc}", wg_lo + kc * (DFF // 2), DFF // 2)
        A.sub(f"wu{kc}", wu_lo + kc * (DFF // 2), DFF // 2)
    assert A.off == TOTAL, A.off
    print("arena: L_END", L_END, "WA_END", WA_END, "CA_END", CA_END, "WBASE", WBASE)
    WG_ALL = [f"wg{kc}" for kc in range(8)]
    WU_ALL = [f"wu{kc}" for kc in range(8)]
    WD_ALL = [f"wd{fc}" for fc in range(NFC)]

    winf3 = winf.rearrange("p (k n) -> p k n", k=8)
    win_src = win_d.rearrange("(k p) n -> p k n", p=128)
    dma("sp", g1col, g1_d, ["d:const"], ["g1col"])
    for kc in range(8):
        dma("sp", winf3[:, kc, :], win_src[:, kc, :], ["d:w_in"], [f"winf{kc}"], nbytes=128 * 896 * 4, grp=("win", 8))
    dma("sp", ident, ident_d, ["d:const"], ["ident"])
    dma("sp", cmask, cmask_d, ["d:const"], ["cmask"])
    dma("sp", kod, kod_d, ["d:const"], ["kod"])
    dma("sp", cvec, cvec_d, ["d:const"], ["cvec"])
    dma("sp", cmask2, cm2_d, ["d:const"], ["cmask2"])
    dma("sp", gret, gret_d, ["d:const"], ["gret"])
    dma("sp", gqk, gqk_d, ["d:const"], ["gqk"])
    dma("sp", cos_t, cos_d, ["d:const"], ["cos_t"], nbytes=128 * NT * 32 * 4)
    dma("sp", sin_t, sin_d, ["d:const"], ["sin_t"], nbytes=128 * NT * 64 * 4)
    dma("sp", gsub, gsub_d, ["d:const"], ["gsub"])
    dma("sp", lam4, lam_d, ["d:const"], ["lam4"])
    for kc in range(8):
        if kc % 2 == 0:
            S.add("dve", lambda e, kc=kc: e.tensor_scalar(out=win3[:, kc, :], in0=winf3[:, kc, :], scalar1=g1col[:, kc:kc + 1],
                                                         scalar2=None, op0=ALU.mult), reads=[f"winf{kc}", "g1col"], writes=["win"], cost=cD(896))
        else:
            S.add("act", lambda e, kc=kc: e.activation(out=win3[:, kc, :], in_=winf3[:, kc, :], func=AF.Copy, scale=g1col[:, kc:kc + 1]),
                  reads=[f"winf{kc}", "g1col"], writes=["win"], cost=cA(896))

    S.add("pool", lambda e: e.memset(neghalf, -0.5), writes=["small"], cost=150)
    S.add("pool", lambda e: e.memset(St, 0.0), writes=["St"], cost=150)
    S.add("pool", lambda e: e.memset(Sb, 0.0), writes=["Sb"], cost=150)
    for i in range(NS):
        S.add("pool", lambda e, i=i: e.memset(qb[i], 0.0), writes=[f"qb{i}"], cost=300)
        S.add("pool", lambda e, i=i: e.memset(o_sb[i].rearrange("p (h c) -> p h c", h=2)[:, :, 64:65], math.sqrt(64 * EPS)),
              writes=[f"o_sb{i}"], cost=150)
        dma("sp", comb[i][:, 4:8], kod_d, ["d:const"], [f"comb{i}"])
    for i in range(NX):
        S.add("pool", lambda e, i=i: e.memset(xt[i][:, DM:DM + 1], 32.0 * math.sqrt(EPS)), writes=[f"xt{i}"], cost=150)

    S.add("pool", lambda e: e.memset(vd3[:, :, 128:130], 1.0), writes=[f"vd{t}" for t in range(NT)], cost=500)
    l4 = lam4.rearrange("p (a b) -> p a b", b=64)
    lt3 = lamtmp.rearrange("p (a b) -> p a b", b=64)
    S.add("dve", lambda e: e.tensor_tensor(out=lt3[:, 0:2, :], in0=l4[:, 0:2, :], in1=l4[:, 2:4, :], op=ALU.mult),
          reads=["lam4"], writes=["lamtmp"])
    S.add("dve", lambda e: e.tensor_reduce(out=lsum, in_=lt3[:, 0:2, :], axis=AX.X, op=ALU.add),
          reads=["lamtmp"], writes=["small"])
    S.add("act", lambda e: e.activation(out=lam_e, in_=lsum, func=AF.Exp), reads=["small"], writes=["small"])
    S.add("dve", lambda e: e.tensor_tensor(out=neglam, in0=lam_e[:, 1:2], in1=lam_e[:, 0:1], op=ALU.subtract),
          reads=["small"], writes=["small"])
    S.add("dve", lambda e: e.tensor_scalar(out=neglam, in0=neglam, scalar1=-LAMBDA_INIT, scalar2=None, op0=ALU.add),
          reads=["small"], writes=["small"])
    gq3 = gqk.rearrange("p (a b) -> p a b", b=128)
    S.add("dve", lambda e: e.tensor_reduce(out=gmax, in_=gq3, axis=AX.X, op=ALU.max, apply_absolute_value=True),
          reads=["gqk"], writes=["small"])
    S.add("dve", lambda e: e.tensor_tensor(out=negM, in0=gmax[:, 0:1], in1=gmax[:, 1:2], op=ALU.mult),
          reads=["small"], writes=["small"])
    S.add("dve", lambda e: e.tensor_scalar(out=negM, in0=negM, scalar1=-8.0, scalar2=None, op0=ALU.mult),
          reads=["small"], writes=["small"])
    S.add("dve", lambda e: e.tensor_scalar(out=gsub, in0=gsub, scalar1=1.0 - LAMBDA_INIT, scalar2=None, op0=ALU.mult),
          reads=["gsub"], writes=["gsub"])
    S.add("dve", lambda e: e.tensor_scalar(out=gqk, in0=gqk, scalar1=8.0, scalar2=None, op0=ALU.mult), reads=["gqk"], writes=["gqk"],
          cost=cD(256))
    S.add("dve", lambda e: e.tensor_scalar(out=gret, in0=gret, scalar1=8.0, scalar2=None, op0=ALU.mult), reads=["gret"], writes=["gret"],
          cost=cD(128))

    B_XT, B_A, B_B, B_T2, B_S0, B_S1, B_O, B_A2 = 0, 1, 2, 7, 4, 5, 6, 3
    B_T3 = B_T2
    psT = bank_bf(B_XT)
    psA = bank(B_A)
    psA2 = bank(B_A2)
    psB = bank(B_B)
    psT2 = bank_bf(B_T2)
    psT3 = bank_bf(B_T2)[:, 384:640]
    psS = [bank(B_S0), bank(B_S1)]
    psO = bank(B_O)
    psMX = bank_bf(B_S1)
    cos3 = cos_t.rearrange("p (t f) -> p t f", f=32)
    sin4 = sin_t.rearrange("p (t h f) -> p t h f", h=2, f=32)

    def phaseA_tile(t):
        s = t % 2
        sx = t % NX
        u = t % NS
        G = t // 4
        dma("sp", xt[sx][:, 0:DM], x_d[t * 128:(t + 1) * 128, :], ["d:x"] + ([f"done{(t - RUNAHEAD) % 8}"] if t >= RUNAHEAD else []),
            [f"xt{sx}"], nbytes=128 * 4096, grp=("x", NX))
        S.add("act", lambda e: e.activation(out=junk[:, 0:DM + 1], in_=xt[sx], func=AF.Square, scale=1.0 / 32.0, accum_out=ss[u]),
              reads=[f"xt{sx}"], writes=["junk", f"ss{u}"], cost=cA(1025, True))
        sb_ = t % NXB
        if XCAST_DMA:
            dma("pool", xb[sb_], xt[sx][:, 0:DM], [f"xt{sx}"], [f"xb{sb_}"], nbytes=128 * 4096, grp=("xc", NXB))
        else:
            S.add("act", lambda e: e.activation(out=xb[sb_], in_=xt[sx][:, 0:DM], func=AF.Copy), reads=[f"xt{sx}"], writes=[f"xb{sb_}"],
                  cost=cA(1024))
        S.add("pool", lambda e: e.tensor_tensor(out=rstd[u], in0=ss[u], in1=neghalf, op=ALU.pow),
              reads=[f"ss{u}", "small"], writes=[f"rstd{u}"], cost=cPow(1))
        rs = rstd[u]

        def tr_x(e):
            ins = None
            for kc in range(8):
                ins = e.transpose(psT[:, kc * 128:(kc + 1) * 128], xb[sb_][:, kc * 128:(kc + 1) * 128], ident)
            return ins
        S.add("pe", tr_x, reads=[f"xb{sb_}", "ident"], writes=[f"ps{B_XT}"], cost=8 * 75)
        S.add("dve", lambda e: e.tensor_copy(out=xT[s], in_=psT), reads=[], writes=[f"xT{s}"], xreads=[f"ps{B_XT}"], cost=cD(1024, fast=True))
        xT3 = xT[s].rearrange("p (k n) -> p k n", k=8)

        def inprojA(e):
            ins = None
            for kc in range(8):
                ins = e.matmul(psA[:, 0:256], lhsT=xT3[:, kc, :], rhs=win3[:, kc, 0:256], start=(kc == 0), stop=(kc == 7))
            return ins

        def inprojA2(e):
            ins = None
            for kc in range(8):
                ins = e.matmul(psA2[:, 0:256], lhsT=xT3[:, kc, :], rhs=win3[:, kc, 256:512], start=(kc == 0), stop=(kc == 7))
            return ins

        def inprojB(e):
            ins = None
            for kc in range(8):
                ins = e.matmul(psB[:, 0:384], lhsT=xT3[:, kc, :], rhs=win3[:, kc, 512:896], start=(kc == 0), stop=(kc == 7))
            return ins
        S.add("pe", inprojA, reads=[f"xT{s}", "win"], writes=[f"ps{B_A}"], cost=cPE([256] * 8))
        S.add("pe", inprojA2, reads=[f"xT{s}", "win"], writes=[f"ps{B_A2}"], cost=cPE([256] * 8))
        S.add("pe", inprojB, reads=[f"xT{s}", "win"], writes=[f"ps{B_B}"], cost=cPE([384] * 8))
        S.add("act", lambda e: e.activation(out=qkraw[u], in_=psB[:, 0:256], func=AF.Copy), reads=[], writes=[f"qkraw{u}"], xreads=[f"ps{B_B}"],
              cost=cA(256))
        S.add("pool", lambda e: e.tensor_tensor(out=sqd[u], in0=qkraw[u], in1=qkraw[u], op=ALU.mult), reads=[f"qkraw{u}"], writes=[f"sqd{u}"],
              cost=cP(256))
        S.add("act", lambda e: e.activation(out=vd3[:, t, 0:128], in_=psB[:, 256:384], func=AF.Copy, scale=rs),
              reads=[f"rstd{u}"], writes=[f"vd{t}"], xreads=[f"ps{B_B}"], cost=cA(128))
        S.add("dve", lambda e: e.tensor_reduce(out=msd[u], in_=sqd[u].rearrange("p (a d) -> p a d", d=64), axis=AX.X, op=ALU.add),
              reads=[f"sqd{u}"], writes=[f"msd{u}"], cost=410)
        S.add("dve", lambda e: e.reciprocal(out=nrm[u][:, 0:1], in_=ss[u]), reads=[f"ss{u}"], writes=[f"nrm{u}"], cost=160)
        S.add("dve", lambda e: e.tensor_scalar(out=nrm[u][:, 1:5], in0=msd[u], scalar1=nrm[u][:, 0:1], scalar2=64.0 * EPS,
                                               op0=ALU.mult, op1=ALU.add), reads=[f"msd{u}", f"nrm{u}"], writes=[f"nrm{u}"], cost=cD(4))
        S.add("pool", lambda e: e.tensor_tensor(out=comb[u][:, 0:4], in0=nrm[u][:, 1:5], in1=neghalf.broadcast_to([128, 4]), op=ALU.pow),
              reads=[f"nrm{u}", "small"], writes=[f"comb{u}"], cost=cPow(4))
        S.add("dve", lambda e: e.tensor_scalar(out=sk[u], in0=comb[u], scalar1=rs, scalar2=None, op0=ALU.mult),
              reads=[f"comb{u}", f"rstd{u}"], writes=[f"sk{u}"], cost=cD(8))
        X8 = psA[:, 0:256].rearrange("p (a f) -> p a f", f=32)
        X4 = psA[:, 0:256].rearrange("p (a h f) -> p a h f", h=2, f=32)
        t1_8 = t1[u].rearrange("p (a f) -> p a f", f=32)
        t2h = t2[u].rearrange("p (h a f) -> p h a f", h=2, f=32)
        t2v = t2[u].rearrange("p (h a f) -> p a h f", h=2, f=32)
        cb = cos3[:, t, :].unsqueeze(1).broadcast_to([128, 8, 32])
        nsb = sin4[:, t, 0, :].unsqueeze(1).broadcast_to([128, 4, 32])
        psb = sin4[:, t, 1, :].unsqueeze(1).broadcast_to([128, 4, 32])
        S.add("dve", lambda e: e.tensor_tensor(out=t1_8, in0=X8, in1=cb, op=ALU.mult),
              reads=["cos_t"], writes=[f"t1{u}"], xreads=[f"ps{B_A}"], cost=cD(256))
        S.add("dve", lambda e: e.tensor_tensor(out=t2h[:, 0, :, :], in0=X4[:, :, 1, :], in1=nsb, op=ALU.mult),
              reads=["sin_t"], writes=[f"t2a{u}"], xreads=[f"ps{B_A}"], cost=cD(128))
        S.add("dve", lambda e: e.tensor_tensor(out=t2h[:, 1, :, :], in0=X4[:, :, 0, :], in1=psb, op=ALU.mult),
              reads=["sin_t"], writes=[f"t2b{u}"], xreads=[f"ps{B_A}"], cost=cD(128))
        S.add("act", lambda e: e.activation(out=vb[u], in_=psA2[:, 0:128], func=AF.Copy, scale=rs),
              reads=[f"rstd{u}"], writes=[f"vb{u}"], xreads=[f"ps{B_A2}"], cost=cA(128))
        S.add("act", lambda e: e.activation(out=gate[u], in_=psA2[:, 128:256], func=AF.Silu, scale=rs),
              reads=[f"rstd{u}"], writes=[f"gate{u}"], xreads=[f"ps{B_A2}"], cost=cA(128))
        t1v = t1[u].rearrange("p (a h f) -> p a h f", h=2, f=32)
        S.add("dve", lambda e: e.tensor_tensor(out=qb[u].rearrange("p (a h f) -> p a h f", h=2, f=32)[:, 0:4:3, :, :],
                                               in0=t1v[:, 0:2, :, :], in1=t2v[:, 0:2, :, :], op=ALU.add),
              reads=[f"t1{u}", f"t2a{u}", f"t2b{u}"], writes=[f"qb{u}"], cost=cD(128))
        S.add("pool", lambda e: e.tensor_tensor(out=rk[u].rearrange("p (a h f) -> p a h f", h=2, f=32), in0=t1v[:, 2:4, :, :],
                                                in1=t2v[:, 2:4, :, :], op=ALU.add),
              reads=[f"t1{u}", f"t2a{u}", f"t2b{u}"], writes=[f"rk{u}"], cost=cP(128))
        S.add("dve", lambda e: e.tensor_tensor(out=kb[u].rearrange("p (h d) -> p h d", h=2), in0=rk[u].rearrange("p (h d) -> p h d", h=2),
                                               in1=sk[u][:, 4:6].unsqueeze(2).broadcast_to([128, 2, 64]), op=ALU.mult),
              reads=[f"rk{u}", f"sk{u}"], writes=[f"kb{u}"], cost=cD(128))
        S.add("pool", lambda e: e.tensor_tensor(out=gg[u], in0=gate[u], in1=gret, op=ALU.mult), reads=[f"gate{u}", "gret"],
              writes=[f"gg{u}"], cost=cP(128))

        def tr_qk(e):
            e.transpose(psT2[:, 0:128], qb[u][:, 0:128], ident)
            e.transpose(psT2[:, 128:256], qb[u][:, 128:256], ident)
            return e.transpose(psT2[:, 256:384], kb[u], ident)
        S.add("pe", tr_qk, reads=[f"qb{u}", f"kb{u}", "ident"], writes=[f"ps{B_T2}"], cost=225)
        S.add("act", lambda e: e.activation(out=qkT[u], in_=psT2[:, 0:384], func=AF.Copy), reads=[], writes=[f"qkT{u}"], xreads=[f"ps{B_T2}"],
              cost=cA(384))
        qbd = qkT[u][:, 0:256]
        kT = qkT[u][:, 256:384]
        S.add("pe", lambda e: e.matmul(psS[0][:, 0:256], lhsT=kT, rhs=qbd, start=True, stop=True),
              reads=[f"qkT{u}"], writes=[f"ps{B_S0}"], cost=cPE([256]))
        S.add("dve", lambda e: e.tensor_tensor(out=Mm[u], in0=psS[0][:, 0:256], in1=cmask2, op=ALU.mult),
              reads=["cmask2"], writes=[f"Mm{u}"], xreads=[f"ps{B_S0}"], cost=cD(256))

        def omm(e):
            ins = None
            for h in range(2):
                ins = e.matmul(psO[:, h * 64:(h + 1) * 64], lhsT=Mm[u][:, h * 128:(h + 1) * 128], rhs=vb[u][:, h * 64:(h + 1) * 64],
                               start=True, stop=(t == 0))
                if t > 0:
                    ins = e.matmul(psO[:, h * 64:(h + 1) * 64], lhsT=qkT[u][h * 64:(h + 1) * 64, h * 128:(h + 1) * 128],
                                   rhs=Sb[h * 64:(h + 1) * 64, :], start=False, stop=True)
            for h in range(2):
                ins = e.matmul(psO[h * 64:(h + 1) * 64, 128:192], lhsT=kb[u][:, h * 64:(h + 1) * 64], rhs=vb[u][:, h * 64:(h + 1) * 64],
                               start=True, stop=True)
            return ins
        S.add("pe", omm, reads=[f"Mm{u}", f"vb{u}", f"qkT{u}", "Sb", f"kb{u}"], writes=[f"ps{B_O}"], cost=6 * 75)
        o3f = o_sb[u].rearrange("p (h c) -> p h c", h=2)
        o3 = o3f[:, :, 0:64]
        q3f = osq[u].rearrange("p (h c) -> p h c", h=2)
        S.add("dve", lambda e: e.scalar_tensor_tensor(out=St, in0=St, scalar=cvec, in1=psO[:, 128:192], op0=ALU.mult, op1=ALU.add),
              reads=["St", "cvec"], writes=["St"], xreads=[f"ps{B_O}"], cost=cD(64))
        S.add("pool", lambda e: e.tensor_copy(out=Sb, in_=St), reads=["St"], writes=["Sb"], cost=380)
        S.add("dve", lambda e: e.tensor_tensor(out=o3, in0=psO[:, 0:128].rearrange("p (h e) -> p h e", h=2),
                                               in1=sk[u][:, 6:8].unsqueeze(2).broadcast_to([128, 2, 64]), op=ALU.mult),
              reads=[f"sk{u}"], writes=[f"o_sb{u}"], xreads=[f"ps{B_O}"], cost=cD(128))
        for h in range(2):
            S.add("dve", lambda e, h=h: e.scalar_tensor_tensor(out=q3f[:, h, :], in0=o3f[:, h, :], scalar=1.0, in1=o3f[:, h, :],
                                                                op0=ALU.mult, op1=ALU.mult, accum_out=ss2[u][:, h:h + 1]),
                  reads=[f"o_sb{u}"], writes=[f"osq{u}h{h}", f"ss2{u}h{h}"], cost=cD(65, True))
        S.add("pool", lambda e: e.tensor_tensor(out=r2[u], in0=ss2[u], in1=neghalf.broadcast_to([128, 2]), op=ALU.pow),
              reads=[f"ss2{u}h0", f"ss2{u}h1", "small"], writes=[f"r2{u}"], cost=cPow(2))
        S.add("dve", lambda e: e.tensor_tensor(out=otmp[u].rearrange("p (h e) -> p h e", h=2), in0=o3,
                                               in1=r2[u].unsqueeze(2).broadcast_to([128, 2, 64]), op=ALU.mult),
              reads=[f"o_sb{u}", f"r2{u}"], writes=[f"otmp{u}"], cost=cD(128))
        S.add("pool", lambda e: e.tensor_tensor(out=mixr[u], in0=otmp[u], in1=gg[u], op=ALU.mult), reads=[f"otmp{u}", f"gg{u}"],
              writes=[f"mixr{u}"], cost=cP(128))
        S.add("pe", lambda e: e.transpose(psMX[:, 0:128], mixr[u], ident), reads=[f"mixr{u}", "ident"], writes=[f"ps{B_S1}"], cost=75)
        S.add("act", lambda e: e.activation(out=mretR[G % 2][:, (t % 4) * 128:(t % 4 + 1) * 128], in_=psMX[:, 0:128], func=AF.Copy),
              reads=[], writes=[f"mretR{G % 2}"], xreads=[f"ps{B_S1}"], cost=cA(128))
        if t % 4 == 3 and stop_after is None:
            dma("sp", mixsrc[G * 256: G * 256 + 128, :], mretR[G % 2], [f"mretR{G % 2}"], [f"d:ms{G}"], nbytes=128 * 1024, grp=("ms", 2))
        if t % 4 == 3 and stop_after == "B":
            dma("sp", y_d[0:128, G * 512:(G + 1) * 512], mretR[G % 2], [f"mretR{G % 2}"], ["d:y0"], nbytes=128 * 1024)
        S.add("dve", lambda e: e.tensor_tensor(out=tmpd[u].rearrange("p (a d) -> p a d", d=64),
                                               in0=qkraw[u].rearrange("p (a d) -> p a d", d=64),
                                               in1=sk[u][:, 0:4].unsqueeze(2).broadcast_to([128, 4, 64]), op=ALU.mult),
              reads=[f"sk{u}", f"qkraw{u}"], writes=[f"tmpd{u}"], cost=cD(256))
        S.add("pool", lambda e: e.tensor_tensor(out=qkh[u], in0=tmpd[u], in1=gqk, op=ALU.mult), reads=[f"tmpd{u}", "gqk"],
              writes=[f"qkh{u}"], cost=cP(256))

        def tr_qkh(e):
            e.transpose(psT3[:, 0:128], qkh[u][:, 0:128], ident)
            return e.transpose(psT3[:, 128:256], qkh[u][:, 128:256], ident)
        S.add("pe", tr_qkh, reads=[f"qkh{u}", "ident"], writes=[f"ps{B_T3}"], cost=150)
        S.add("act", lambda e: e.activation(out=qdT[:, t * 128:(t + 1) * 128], in_=psT3[:, 0:128], func=AF.Copy),
              reads=[], writes=[f"qdT{t}"], xreads=[f"ps{B_T3}"], cost=cA(128))
        S.add("act", lambda e: e.activation(out=kdT[:, t * 128:(t + 1) * 128], in_=psT3[:, 128:256], func=AF.Copy),
              reads=[], writes=[f"kdT{t}", f"done{t % 8}"], xreads=[f"ps{B_T3}"], cost=cA(128))

    S.tag = "A"
    for t in range(NT):
        S.tile = t
        phaseA_tile(t)
    S.tile = ""
    S.tag = "Wpre"
    wo3 = wo_b.rearrange("p (k n) -> p k n", k=8)
    wg3 = wg_b.rearrange("p (k n) -> p k n", k=8)
    wu3 = wu_b.rearrange("p (k n) -> p k n", k=8)
    wd3 = wd_b.rearrange("p (k n) -> p k n", k=NFC)
    wo_src = wout_d.rearrange("(k p) n -> p k n", p=128)
    wg_src = wg_d.rearrange("(k p) n -> p k n", p=128)
    wu_src = wu_d.rearrange("(k p) n -> p k n", p=128)
    wd_src = wd_d.rearrange("(k p) n -> p k n", p=128)
    if stop_after is None:
        for kc in range(0, 8, 2):
            dma("pool", wo3[:, kc:kc + 2, :], wo_src[:, kc:kc + 2, :], ["d:w_out"], ["wo"], nbytes=128 * 2048 * 4, grp=("w", 4))
        for kc in range(8):
            dma("pool", wg3[:, kc, :], wg_src[:, kc, :], ["d:w_gate"], [f"wg{kc}"], nbytes=128 * DFF * 4, grp=("w", 4))
        WD_LATE = [fc for fc in range(NFC) if WBASE + (NFC - 1 - fc) * 512 < L_END]
        for fc in range(NFC):
            if fc not in WD_LATE:
                dma("pool", wd3[:, NFC - 1 - fc, :], wd_src[:, fc, :], ["d:w_down"], [f"wd{fc}"], nbytes=128 * 1024 * 4, grp=("w", 4))
        for kc in range(5):
            dma("pool", wu3[:, kc, :], wu_src[:, kc, :], ["d:w_up"], [f"wu{kc}"], nbytes=128 * DFF * 4, grp=("w", 4))
    WU_LATE = {4: 5, 10: 6, 15: 7}

    SB = [(0, 1), (2, 3)]
    OB = (4, 5, 6)
    B_TD = 7
    psTD = bank_bf(B_TD)

    def acc_ap(a, c0, c1):
        bk = OB[a // 3]
        base = (a % 3) * 130
        return PS[:, bk * 512 + base + c0: bk * 512 + base + c1]

    def phaseB_group(G):
        njt = 4 * G + 4
        pend = None
        for jt in range(njt):
            buf = jt % 2
            dl = jt - 4 * G
            il0 = max(dl, 0)
            c0 = il0 * 128
            b0, b1 = SB[buf]

            def qk(e, jt=jt, c0=c0, b0=b0, b1=b1):
                e.matmul(PS[:, b0 * 512 + c0: b0 * 512 + 512], lhsT=kdT[0:64, jt * 128:(jt + 1) * 128],
                         rhs=qdT[0:64, G * 512 + c0: G * 512 + 512], start=True, stop=True)
                return e.matmul(PS[:, b1 * 512 + c0: b1 * 512 + 512], lhsT=kdT[64:128, jt * 128:(jt + 1) * 128],
                                rhs=qdT[64:128, G * 512 + c0: G * 512 + 512], start=True, stop=True)
            S.add("pe", qk, reads=[f"kdT{jt}"] + [f"qdT{4 * G + i}" for i in range(il0, 4)], writes=[f"ps{b0}", f"ps{b1}"],
                  cost=cPE([512 - c0]) + 10)
            s_in = PS[:, b0 * 512: b0 * 512 + 1024].rearrange("p (m i) -> p m i", m=2)[:, :, c0:512]
            p_out = PT[buf].rearrange("p (m i) -> p m i", m=2)[:, :, c0:512]
            S.add("act", lambda e, s_in=s_in, p_out=p_out: e.activation(out=p_out, in_=s_in, func=AF.Exp, bias=negM, scale=0.125),
                  reads=["small"], writes=[f"PT{buf}"], xreads=[f"ps{b0}", f"ps{b1}"], cost=cA(2 * (512 - c0)))
            if dl >= 0:
                blk = PT[buf].rearrange("p (m i) -> p m i", m=2)[:, :, dl * 128:(dl + 1) * 128]
                S.add("dve", lambda e, blk=blk: e.tensor_tensor(out=blk, in0=blk, in1=cmask.unsqueeze(1).broadcast_to([128, 2, 128]),
                                                               op=ALU.mult), reads=["cmask", f"PT{buf}"], writes=[f"PT{buf}"], cost=cD(256))
            if pend is not None:
                pend()

            def pv_decl(jt=jt, il0=il0, buf=buf):
                def pv(e):
                    ins = None
                    for il in range(il0, 4):
                        for m in range(2):
                            a = il * 2 + m
                            ins = e.matmul(acc_ap(a, 0, 129), lhsT=PT[buf][:, m * 512 + il * 128: m * 512 + (il + 1) * 128],
                                           rhs=vd3[:, jt, 0:129], start=(jt == 0 and a % 3 == 0), stop=(jt == 4 * G + il),
                                           skip_group_check=True)
                    return ins
                S.add("pe", pv, reads=[f"PT{buf}", f"vd{jt}"], writes=[f"ps{b}" for b in OB], cost=cPE([129] * (2 * (4 - il0))))
            pend = pv_decl
        pend()
        O3 = Osb.rearrange("p (a c) -> p a c", c=130)
        for bi, bk in enumerate(OB):
            n = 3 if bi < 2 else 2
            S.add("dve", lambda e, bi=bi, bk=bk, n=n: e.tensor_copy(
                out=Osb[:, bi * 390: bi * 390 + n * 130].rearrange("p (a c) -> p a c", c=130)[:, :, 0:129],
                in_=PS[:, bk * 512: bk * 512 + n * 130].rearrange("p (a c) -> p a c", c=130)[:, :, 0:129]),
                  reads=[], writes=["Osb"], xreads=[f"ps{bk}"], cost=cD(n * 130))
        S.add("dve", lambda e: e.reciprocal(out=rl, in_=O3[:, :, 128]), reads=["Osb"], writes=["rl"], cost=cD(8))
        rl2 = rl.rearrange("p (i m) -> p i m", m=2)
        S.add("dve", lambda e: e.tensor_scalar(out=coef, in0=rl2[:, :, 1], scalar1=neglam, scalar2=None, op0=ALU.mult),
              reads=["rl", "small"], writes=["coef"], cost=cD(4))
        for il in range(4):
            it = 4 * G + il
            S.add("dve", lambda e, il=il: e.tensor_scalar(out=dtmp, in0=O3[:, 2 * il, 0:128], scalar1=rl[:, 2 * il:2 * il + 1],
                                                         scalar2=None, op0=ALU.mult), reads=["Osb", "rl"], writes=["dtmp"], cost=cD(128))
            S.add("dve", lambda e, il=il: e.scalar_tensor_tensor(out=dtmp, in0=O3[:, 2 * il + 1, 0:128], scalar=coef[:, il:il + 1],
                                                                in1=dtmp, op0=ALU.mult, op1=ALU.add),
                  reads=["Osb", "coef", "dtmp"], writes=["dtmp"], cost=cD(128))
            S.add("dve", lambda e: e.scalar_tensor_tensor(out=dsq, in0=dtmp, scalar=1.0, in1=dtmp, op0=ALU.mult, op1=ALU.mult,
                                                          accum_out=ssd), reads=["dtmp"], writes=["dsq", "ssd"], cost=cD(128, True))
            S.add("dve", lambda e: e.tensor_scalar(out=rd, in0=ssd, scalar1=1.0 / 128, scalar2=EPS, op0=ALU.mult, op1=ALU.add),
                  reads=["ssd"], writes=["rd"], cost=cD(1))
            S.add("pool", lambda e: e.tensor_tensor(out=rd, in0=rd, in1=neghalf, op=ALU.pow), reads=["small", "rd"], writes=["rd"], cost=cPow(1))
            S.add("dve", lambda e: e.scalar_tensor_tensor(out=difn, in0=dtmp, scalar=rd, in1=gsub, op0=ALU.mult, op1=ALU.mult),
                  reads=["dtmp", "rd", "gsub"], writes=["difn"], cost=cD(128))
            S.add("pe", lambda e: e.transpose(psTD[:, 0:128], difn, ident), reads=["difn", "ident"], writes=[f"ps{B_TD}"], cost=75)
            S.add("dve", lambda e, il=il: e.tensor_copy(out=mdifR[G % 2][:, il * 128:(il + 1) * 128], in_=psTD[:, 0:128]),
                  reads=[], writes=[f"mdifR{G % 2}"], xreads=[f"ps{B_TD}"], cost=cD(128, fast=True))

    def exchange_group(G):
        dma("sp", mixsrc[G * 256 + 128: G * 256 + 256, :], mdifR[G % 2], [f"mdifR{G % 2}"], [f"d:ms{G}"], nbytes=128 * 1024,
            grp=("ms2", 2))
        S.add("pool", lambda e: e.collective_compute("AllGather", ALU.bypass, replica_groups=[[0, 1, 2, 3], [4, 5, 6, 7]],
                                                     ins=[mixsrc_t.ap()[G * 256:(G + 1) * 256, :].opt()],
                                                     outs=[gath_t.ap()[G * 1024:(G + 1) * 1024, :].opt()]),
              reads=[f"d:ms{G}"], writes=[f"d:gath{G}"], cc=True, cost=1000, lat=30000)

    for G in range(NT // 4):
        S.tag = "B"
        phaseB_group(G)
        if stop_after is None:
            S.tag = "X"
            exchange_group(G)
            if G in WU_LATE:
                kc = WU_LATE[G]
                dma("pool", wu3[:, kc, :], wu_src[:, kc, :], ["d:w_up"], [f"wu{kc}"], nbytes=128 * DFF * 4, grp=("w", 4))
        else:
            dma("sp", y_d[128:256, G * 512:(G + 1) * 512], mdifR[G % 2], [f"mdifR{G % 2}"], ["d:y1"], nbytes=128 * 1024)

    if stop_after == "B":
        S.add("sp", lambda e: e.wait_ge(S.sem["sp"], 0), reads=["d:y0", "d:y1"], writes=[], dma=False, cost=50)
        S.emit(reorder)
        return nc

    S.tag = "Wlate"
    dma("sp", g2col, g2_d, ["d:const"], ["g2col"])
    for fc in WD_LATE:
        dma("pool", wd3[:, NFC - 1 - fc, :], wd_src[:, fc, :], ["d:w_down"], [f"wd{fc}"], nbytes=128 * 1024 * 4, grp=("w", 4))
    NG = TSH // 256
    rank_cache = {}
    PB_DN = (0, 1)
    PB_T = 2
    PB_G = (3, 4)
    PB_U = (5, 6)
    PB_OP = 7
    psTC = bank_bf(PB_T)
    ff3 = ffT.rearrange("p (f t) -> p f t", f=NFC)

    def stageX(g):
        s = g % 2
        X1 = x1[s].rearrange("p (t n) -> p t n", t=2)
        dma("sp", X1, xres_d[g * 256:(g + 1) * 256, :].rearrange("(t p) n -> p t n", p=128), ["d:xres"], [f"x1_{s}"], nbytes=1 << 20,
            grp=("x1", 2))
        mg3 = mixg.rearrange("p (c t) -> p c t", c=8)

        def ld_mix(e):
            if "r" not in rank_cache:
                rank_cache["r"] = e.partition_id() % 4
            src_ap = gath[bass.ds(rank_cache["r"] * 4096 + (g // 2) * 1024, 1024), (g % 2) * 256:(g % 2) * 256 + 256]
            return e.dma_start(out=mg3, in_=src_ap.rearrange("(c p) t -> p c t", p=128))
        S.add("pool", ld_mix, reads=[f"d:gath{r * 4 + g // 2}" for r in range(4)], writes=["mixg"], dma=True, cost=900, lat=6000,
              semgrp=("mixg", 2))
        h2T3 = h2T[s].rearrange("p (k t) -> p k t", k=8)
        for tt in range(2):
            for nh in range(2):
                def op_mm(e, tt=tt, nh=nh):
                    ins = None
                    for c in range(8):
                        ins = e.matmul(bank(PB_OP), lhsT=mg3[:, c, tt * 128:(tt + 1) * 128], rhs=wo3[:, c, nh * 512:(nh + 1) * 512],
                                       start=(c == 0), stop=(c == 7))
                    return ins
                S.add("pe", op_mm, reads=["mixg", "wo"], writes=[f"ps{PB_OP}"], cost=cPE([512] * 8))
                S.add("dve", lambda e, tt=tt, nh=nh: e.tensor_tensor(out=X1[:, tt, nh * 512:(nh + 1) * 512], in0=bank(PB_OP),
                                                                    in1=X1[:, tt, nh * 512:(nh + 1) * 512], op=ALU.add),
                      reads=[f"x1_{s}"], writes=[f"x1_{s}"], xreads=[f"ps{PB_OP}"], cost=cD(512))
        for tt in range(2):
            S.add("act", lambda e, tt=tt: e.activation(out=mixg[:, 0:DM], in_=X1[:, tt, :], func=AF.Square, accum_out=ssc[:, tt:tt + 1]),
                  reads=[f"x1_{s}"], writes=["mixg", "ssc"], cost=cA(1024, True))
            S.add("dve", lambda e, tt=tt: e.tensor_scalar(out=rsc[:, tt:tt + 1], in0=ssc[:, tt:tt + 1], scalar1=1.0 / DM, scalar2=EPS,
                                                         op0=ALU.mult, op1=ALU.add), reads=["ssc"], writes=["rsc"], cost=cD(1))
            S.add("pool", lambda e, tt=tt: e.tensor_tensor(out=rsc[:, tt:tt + 1], in0=rsc[:, tt:tt + 1], in1=neghalf, op=ALU.pow),
                  reads=["small", "rsc"], writes=["rsc"], cost=cPow(1))
            S.add("act", lambda e, tt=tt: e.activation(out=h2, in_=X1[:, tt, :], func=AF.Copy, scale=rsc[:, tt:tt + 1]),
                  reads=[f"x1_{s}", "rsc"], writes=["h2"], cost=cA(1024))

            def tr_h(e):
                ins = None
                for kc in range(8):
                    ins = e.transpose(psTC[:, kc * 128:(kc + 1) * 128], h2[:, kc * 128:(kc + 1) * 128], ident)
                return ins
            S.add("pe", tr_h, reads=["h2", "ident"], writes=[f"ps{PB_T}"], cost=8 * 75)
            S.add("dve", lambda e, tt=tt: e.tensor_tensor(out=h2T3[:, :, tt * 128:(tt + 1) * 128],
                                                         in0=psTC.rearrange("p (k t) -> p k t", k=8),
                                                         in1=g2col.unsqueeze(2).broadcast_to([128, 8, 128]), op=ALU.mult),
                  reads=["g2col"], writes=[f"h2T{s}"], xreads=[f"ps{PB_T}"], cost=cD(1024))

    def stageF(g):
        s = g % 2
        X1 = x1[s].rearrange("p (t n) -> p t n", t=2)
        h2T3 = h2T[s].rearrange("p (k t) -> p k t", k=8)
        for fc in range(NFC):
            pg = PB_G[fc % 2]
            pu = PB_U[fc % 2]
            sgc = sg[fc % 2]

            def gu(e, fc=fc, pg=pg, pu=pu):
                ins = None
                for c in range(8):
                    ins = e.matmul(bank(pg, 0, 256), lhsT=wg3[:, c, fc * 128:(fc + 1) * 128], rhs=h2T3[:, c, :],
                                   start=(c == 0), stop=(c == 7))
                for c in range(8):
                    ins = e.matmul(bank(pu, 0, 256), lhsT=wu3[:, c, fc * 128:(fc + 1) * 128], rhs=h2T3[:, c, :],
                                   start=(c == 0), stop=(c == 7))
                return ins
            S.add("pe", gu, reads=WG_ALL + WU_ALL + [f"h2T{s}"], writes=[f"ps{pg}", f"ps{pu}"], cost=cPE([256] * 16))
            S.add("act", lambda e, pg=pg, sgc=sgc: e.activation(out=sgc, in_=bank(pg, 0, 256), func=AF.Silu),
                  reads=[], writes=[f"sg{fc % 2}"], xreads=[f"ps{pg}"], cost=cA(256))
            S.add("dve", lambda e, fc=fc, pu=pu, sgc=sgc: e.tensor_tensor(out=ff3[:, fc, :], in0=bank(pu, 0, 256), in1=sgc, op=ALU.mult),
                  reads=[f"sg{fc % 2}"], writes=["ffT"], xreads=[f"ps{pu}"], cost=cD(256))
        for tt in range(2):
            for nh in range(2):
                pb = PB_DN[nh]

                def down(e, tt=tt, nh=nh, pb=pb):
                    ins = None
                    for fc in range(NFC):
                        ins = e.matmul(bank(pb), lhsT=ff3[:, fc, tt * 128:(tt + 1) * 128], rhs=wd3[:, NFC - 1 - fc, nh * 512:(nh + 1) * 512],
                                       start=(fc == 0), stop=(fc == NFC - 1))
                    return ins
                S.add("pe", down, reads=["ffT"] + WD_ALL, writes=[f"ps{pb}"], cost=cPE([512] * NFC))
                S.add("dve", lambda e, tt=tt, nh=nh, pb=pb: e.tensor_tensor(out=X1[:, tt, nh * 512:(nh + 1) * 512], in0=bank(pb),
                                                                           in1=X1[:, tt, nh * 512:(nh + 1) * 512], op=ALU.add),
                      reads=[f"x1_{s}"], writes=[f"x1_{s}"], xreads=[f"ps{pb}"], cost=cD(512))
        dma("sp", y_d[g * 256:(g + 1) * 256, :].rearrange("(t p) n -> p t n", p=128), X1, [f"x1_{s}"], [f"d:y{g}"], nbytes=1 << 20,
            grp=("y", 2))

    S.tag = "C"
    stageX(0)
    for g in range(NG):
        if g + 1 < NG:
            stageX(g + 1)
        stageF(g)
    S.add("sp", lambda e: e.wait_ge(S.sem["sp"], 0), reads=[f"d:y{g}" for g in range(8)], writes=[], dma=False, cost=50)
    S.emit(reorder)
    print("sched estimate (us):", S.est_ns / 1e3 if hasattr(S, "est_ns") else None)
    return nc


def _const_tables():
    pos = np.arange(SEQ, dtype=np.float32)
    freqs = (1.0 / (10000.0 ** (np.arange(0, 64, 2, dtype=np.float32) / np.float32(64)))).astype(np.float32)
    ang = (pos[:, None] * freqs[None, :]).astype(np.float32)
    cos = np.cos(ang.astype(np.float64)).astype(np.float32)
    sin = np.sin(ang.astype(np.float64)).astype(np.float32)
    cos_t = cos.reshape(NT, 128, 32).transpose(1, 0, 2).reshape(128, NT * 32)
    ns = np.stack([-sin, sin], axis=1)
    sin_t = ns.reshape(NT, 128, 2, 32).transpose(1, 0, 2, 3).reshape(128, NT * 64)
    ident = np.eye(128, dtype=np.float32).astype(ml_dtypes.bfloat16)
    j = np.arange(128)
    cmask = (j[:, None] <= j[None, :]).astype(np.float32).astype(ml_dtypes.bfloat16)
    return np.ascontiguousarray(cos_t), np.ascontiguousarray(sin_t), ident, cmask


def _decay_tables(hg):
    idx = np.arange(128, dtype=np.float64)
    j = np.arange(128)
    causal = (j[:, None] <= j[None, :]).astype(np.float64)
    kod = np.zeros((128, 4), np.float32)
    cvec = np.zeros((128, 1), np.float32)
    cm2 = np.zeros((128, 256), np.float32)
    for hl in range(2):
        h = 2 * hg + hl
        lg = math.log(1.0 - 2.0 ** (-5.0 - h))
        c = math.exp(lg * 128.0)
        kod[:, hl] = np.exp(-lg * (idx + 1.0)) * (64 ** -0.5) * c
        kod[:, 2 + hl] = np.exp(lg * (idx + 1.0))
        cvec[hl * 64:(hl + 1) * 64, 0] = c
        cm2[:, hl * 128:(hl + 1) * 128] = causal / c
    return kod, cvec, cm2


_NC_CACHE = {}


def _prep_inputs(inputs):
    f = lambda a: np.ascontiguousarray(np.asarray(a, dtype=np.float32))
    x = f(inputs["x"])
    w_in = f(inputs["w_in"])[0]
    w_out = f(inputs["w_out"])[0]
    w_gate = f(inputs["w_gate"])[0]
    w_up = f(inputs["w_up"])[0]
    w_down = f(inputs["w_down"])[0]
    rep = lambda v, n=128: np.ascontiguousarray(np.broadcast_to(np.asarray(v, np.float32).reshape(1, -1), (n, np.asarray(v).size)))
    g1col = np.ascontiguousarray(np.asarray(inputs["norm1_g"][0], np.float32).reshape(8, 128).T)
    g2col = np.ascontiguousarray(np.asarray(inputs["norm2_g"][0], np.float32).reshape(8, 128).T)
    gq = np.asarray(inputs["diff_q_norm_g"][0], np.float32)
    gk = np.asarray(inputs["diff_k_norm_g"][0], np.float32)
    gqk = rep(np.concatenate([gq, gq, gk, gk]))
    gsub = rep(inputs["diff_subln_g"][0])
    lam4 = rep(np.concatenate([np.asarray(inputs[k][0], np.float32) for k in ("lambda_q1", "lambda_q2", "lambda_k1", "lambda_k2")]))
    cos_t, sin_t, ident, cmask = _const_tables()
    in_maps = []
    for c in range(NCORES):
        b, hg = divmod(c, 4)
        cols = []
        for base in (0, 512, 1024, 1536):
            cols += list(range(base + hg * 128, base + hg * 128 + 128))
        for base in (2048, 2560, 3072):
            cols += list(range(base + hg * 128, base + hg * 128 + 128))
        rows = []
        for r in range(4):
            rows += list(range(r * 128, r * 128 + 128)) + list(range(512 + r * 128, 512 + r * 128 + 128))
        kod, cvec, cm2 = _decay_tables(hg)
        gret = rep(np.asarray(inputs["ret_norm_g"][0], np.float32)[2 * hg:2 * hg + 2].reshape(-1))
        in_maps.append({
            "x": x[b],
            "xres": np.ascontiguousarray(x[b, hg * TSH:(hg + 1) * TSH]),
            "w_in": np.ascontiguousarray(w_in[:, cols]),
            "w_out": np.ascontiguousarray(w_out[rows, :]),
            "w_gate": w_gate, "w_up": w_up, "w_down": w_down,
            "g1col": g1col, "g2col": g2col,
            "cos_t": cos_t, "sin_t": sin_t,
            "kod": kod, "cvec": cvec, "cmask2": cm2,
            "gret": gret, "gqk": gqk, "gsub": gsub, "lam4": lam4,
            "ident": ident, "cmask": cmask,
        })
    return in_maps


def kernel(**inputs):
    in_maps = _prep_inputs(inputs)
    if "nc" not in _NC_CACHE:
        _NC_CACHE["nc"] = build_nc()
    nc = _NC_CACHE["nc"]
    res = run_bass_kernel_spmd(nc, in_maps, core_ids=list(range(NCORES)))
    out = np.empty((2, SEQ, DM), np.float32)
    for c in range(NCORES):
        b, hg = divmod(c, 4)
        out[b, hg * TSH:(hg + 1) * TSH] = res.results[c]["y"]
    return out
```
